# Optimizing a Trainium2 kernel written in Bass

```python
import jax
import jax.numpy as jnp
from jax import lax
import numpy as np

D_MODEL = 1024
BATCH = 8
SEQ = 4096
DEPTH = 2
DEC_BATCH = 32
DEC_SEQ = 16
PAST_LEN = 4096

CHUNK = 64
Q_BLOCK = 128
N_HEADS = 8
QK_NOPE = 64
QK_ROPE = 32
QK_HEAD = QK_NOPE + QK_ROPE
V_HEAD = 64
Q_LORA = 384
KV_LORA = 256
ATTN_WIDTH = N_HEADS * V_HEAD
POOL_WINDOWS = (2, 4, 8, 16)
N_POOL_GROUPS = 4
POOL_GROUP = 128
POOL_WIDTH = N_POOL_GROUPS * POOL_GROUP
POOL_HIST = max(POOL_WINDOWS) - 1
D_FF = 4 * D_MODEL
ROPE_THETA = 10000.0
EPS = 1e-6
SM_SCALE = QK_HEAD ** -0.5

OFF_Q = 0
OFF_KV = OFF_Q + Q_LORA
OFF_KR = OFF_KV + KV_LORA
OFF_P = OFF_KR + QK_ROPE
OFF_GA = OFF_P + POOL_WIDTH
OFF_GB = OFF_GA + D_MODEL
IN_WIDTH = OFF_GB + D_MODEL

kernel_name = 'hybrid_mla_pool_stream_step'


def rmsnorm(x, g):
    xf = x.astype(jnp.float32)
    y = xf * lax.rsqrt(jnp.mean(xf * xf, axis=-1, keepdims=True) + EPS)
    return (y * g.astype(jnp.float32)).astype(x.dtype)


def rope(x, pos):
    half = QK_ROPE // 2
    inv = jnp.power(ROPE_THETA, -jnp.arange(half, dtype=jnp.float32) / half)
    ang = pos[:, None] * inv[None, :]
    shape = (1, ang.shape[0]) + (1,) * (x.ndim - 3) + (half,)
    cos = jnp.cos(ang).reshape(shape)
    sin = jnp.sin(ang).reshape(shape)
    xf = x.astype(jnp.float32)
    x1, x2 = xf[..., :half], xf[..., half:]
    return jnp.concatenate([x1 * cos - x2 * sin, x1 * sin + x2 * cos], axis=-1).astype(x.dtype)


def mla_keys_values(c, kr, w_ukv, g_k):
    kv = jnp.einsum('btc,chd->bthd', c, w_ukv)
    k_nope, v = kv[..., :QK_NOPE], kv[..., QK_NOPE:]
    kr_h = jnp.broadcast_to(kr[:, :, None, :], k_nope.shape[:3] + (QK_ROPE,)).astype(k_nope.dtype)
    k = rmsnorm(jnp.concatenate([k_nope, kr_h], axis=-1), g_k)
    return k, v


def mla_queries(cq, w_uq, g_q, pos):
    q = jnp.einsum('btc,chd->bthd', cq, w_uq)
    q = jnp.concatenate([q[..., :QK_NOPE], rope(q[..., QK_NOPE:], pos)], axis=-1)
    return rmsnorm(q, g_q) * SM_SCALE


def attend_prompt(q, k, v):
    B, S = q.shape[0], q.shape[1]
    nb = S // Q_BLOCK
    qb = q.reshape(B, nb, Q_BLOCK, N_HEADS, QK_HEAD).transpose(1, 0, 2, 3, 4)
    k_chunk = jnp.arange(S) // CHUNK

    def block(args):
        i, qi = args
        q_chunk = (i * Q_BLOCK + jnp.arange(Q_BLOCK)) // CHUNK
        mask = k_chunk[None, :] <= q_chunk[:, None]
        s = jnp.einsum('bqhd,bkhd->bhqk', qi, k).astype(jnp.float32)
        s = jnp.where(mask[None, None], s, -jnp.inf)
        p = jax.nn.softmax(s, axis=-1).astype(v.dtype)
        return jnp.einsum('bhqk,bkhd->bqhd', p, v)

    out = lax.map(block, (jnp.arange(nb), qb))
    return out.transpose(1, 0, 2, 3, 4).reshape(B, S, ATTN_WIDTH)


def attend_all(q, k, v):
    B, T = q.shape[0], q.shape[1]
    s = jnp.einsum('bqhd,bkhd->bhqk', q, k).astype(jnp.float32)
    p = jax.nn.softmax(s, axis=-1).astype(v.dtype)
    return jnp.einsum('bhqk,bkhd->bqhd', p, v).reshape(B, T, ATTN_WIDTH)


def pool_mix(p_ext, n_hist, pos0):
    B, L, W = p_ext.shape
    T = L - n_hist
    pf = p_ext.astype(jnp.float32)
    cs = jnp.concatenate([jnp.zeros((B, 1, W), jnp.float32), jnp.cumsum(pf, axis=1)], axis=1)
    t = jnp.arange(n_hist, L)
    pos = pos0 + jnp.arange(T)
    hi = cs[:, t + 1]
    outs = []
    for g, w in enumerate(POOL_WINDOWS):
        sl = slice(g * POOL_GROUP, (g + 1) * POOL_GROUP)
        lo_idx = jnp.maximum(t + 1 - w, 0)
        cnt = jnp.minimum(w, pos + 1).astype(jnp.float32)
        mean = (hi[..., sl] - cs[:, lo_idx, sl]) / cnt[None, :, None]
        outs.append(mean - pf[:, n_hist:, sl])
    return jnp.concatenate(outs, axis=-1).astype(p_ext.dtype)


def trunk_layer(x, pos0, hist_c, hist_kr, hist_p, g_mix, w_in, g_qa, g_kva, w_uq, w_ukv,
                g_q, g_k, w_attn_out, w_pool, pool_scale, w_pool_out, w_o, g_mlp, w_up, w_down):
    B, T, _ = x.shape
    pos = pos0 + jnp.arange(T, dtype=jnp.float32)
    h = rmsnorm(x, g_mix)
    z = jnp.einsum('btd,de->bte', h, w_in)
    cq = rmsnorm(z[..., OFF_Q:OFF_KV], g_qa)
    ckv = rmsnorm(z[..., OFF_KV:OFF_KR], g_kva)
    kr = rope(z[..., OFF_KR:OFF_P], pos)
    p = z[..., OFF_P:OFF_GA]
    gate_a = jax.nn.sigmoid(z[..., OFF_GA:OFF_GB])
    gate_b = jax.nn.sigmoid(z[..., OFF_GB:IN_WIDTH])
    q = mla_queries(cq, w_uq, g_q, pos)
    if hist_c is None:
        k, v = mla_keys_values(ckv, kr, w_ukv, g_k)
        attn = attend_prompt(q, k, v)
        p_ext, n_hist = p, 0
    else:
        c_all = jnp.concatenate([hist_c.astype(ckv.dtype), ckv], axis=1)
        kr_all = jnp.concatenate([hist_kr.astype(kr.dtype), kr], axis=1)
        k, v = mla_keys_values(c_all, kr_all, w_ukv, g_k)
        attn = attend_all(q, k, v)
        p_ext, n_hist = jnp.concatenate([hist_p.astype(p.dtype), p], axis=1), POOL_HIST
    pooled = pool_mix(p_ext, n_hist, pos0)
    u = jnp.einsum('btgc,gce->btge', pooled.reshape(B, T, N_POOL_GROUPS, POOL_GROUP), w_pool)
    u = u.reshape(B, T, POOL_WIDTH) * pool_scale
    branch_a = jnp.einsum('btc,cd->btd', attn, w_attn_out)
    branch_b = jnp.einsum('btc,cd->btd', u, w_pool_out)
    x = x + jnp.einsum('btd,de->bte', gate_a * branch_a + gate_b * branch_b, w_o)
    hm = rmsnorm(x, g_mlp)
    a = jnp.square(jax.nn.relu(jnp.einsum('btd,df->btf', hm, w_up)))
    x = x + jnp.einsum('btf,fd->btd', a, w_down)
    return x, ckv, kr, p_ext[:, -POOL_HIST:]


def setup_inputs(seed: int = 0) -> dict:
    key = jax.random.key(seed)
    ks = jax.random.split(key, 24)

    def nrm(k, shape, scale):
        return jax.random.normal(k, shape, jnp.float32) * scale

    def gain(k, shape):
        return 1.0 + 0.05 * jax.random.normal(k, shape, jnp.float32)

    return {
        'x_prompt': nrm(ks[0], (BATCH, SEQ, D_MODEL), 1.0),
        'x_sample': nrm(ks[1], (DEC_BATCH, DEC_SEQ, D_MODEL), 1.0),
        'cache_ckv': nrm(ks[2], (DEPTH, DEC_BATCH, PAST_LEN, KV_LORA), 1.0),
        'cache_krope': nrm(ks[3], (DEPTH, DEC_BATCH, PAST_LEN, QK_ROPE), 1.0),
        'state_pool': nrm(ks[4], (DEPTH, DEC_BATCH, POOL_HIST, POOL_WIDTH), 1.0),
        'g_mix': gain(ks[5], (DEPTH, D_MODEL)),
        'w_in': nrm(ks[6], (DEPTH, D_MODEL, IN_WIDTH), D_MODEL ** -0.5),
        'g_qa': gain(ks[7], (DEPTH, Q_LORA)),
        'g_kva': gain(ks[8], (DEPTH, KV_LORA)),
        'w_uq': nrm(ks[9], (DEPTH, Q_LORA, N_HEADS, QK_HEAD), Q_LORA ** -0.5),
        'w_ukv': nrm(ks[10], (DEPTH, KV_LORA, N_HEADS, QK_NOPE + V_HEAD), KV_LORA ** -0.5),
        'g_q': gain(ks[11], (DEPTH, QK_HEAD)),
        'g_k': gain(ks[12], (DEPTH, QK_HEAD)),
        'w_attn_out': nrm(ks[13], (DEPTH, ATTN_WIDTH, D_MODEL), ATTN_WIDTH ** -0.5),
        'w_pool': nrm(ks[14], (DEPTH, N_POOL_GROUPS, POOL_GROUP, POOL_GROUP), POOL_GROUP ** -0.5),
        'pool_scale': gain(ks[15], (DEPTH, POOL_WIDTH)),
        'w_pool_out': nrm(ks[16], (DEPTH, POOL_WIDTH, D_MODEL), POOL_WIDTH ** -0.5),
        'w_o': nrm(ks[17], (DEPTH, D_MODEL, D_MODEL), D_MODEL ** -0.5),
        'g_mlp': gain(ks[18], (DEPTH, D_MODEL)),
        'w_up': nrm(ks[19], (DEPTH, D_MODEL, D_FF), D_MODEL ** -0.5),
        'w_down': nrm(ks[20], (DEPTH, D_FF, D_MODEL), D_FF ** -0.5),
    }


def reference(x_prompt, x_sample, cache_ckv, cache_krope, state_pool, g_mix, w_in, g_qa, g_kva,
              w_uq, w_ukv, g_q, g_k, w_attn_out, w_pool, pool_scale, w_pool_out, w_o, g_mlp,
              w_up, w_down):
    past = cache_ckv.shape[2]
    yp, ys = x_prompt, x_sample
    ckv_p, kr_p, pool_p, ckv_s, kr_s, pool_s = [], [], [], [], [], []
    for l in range(DEPTH):
        w = (g_mix[l], w_in[l], g_qa[l], g_kva[l], w_uq[l], w_ukv[l], g_q[l], g_k[l],
             w_attn_out[l], w_pool[l], pool_scale[l], w_pool_out[l], w_o[l], g_mlp[l],
             w_up[l], w_down[l])
        yp, c1, r1, s1 = trunk_layer(yp, 0, None, None, None, *w)
        ys, c2, r2, s2 = trunk_layer(ys, past, cache_ckv[l], cache_krope[l], state_pool[l], *w)
        ckv_p.append(c1)
        kr_p.append(r1)
        pool_p.append(s1)
        ckv_s.append(c2)
        kr_s.append(r2)
        pool_s.append(s2)
    return (yp, ys, jnp.stack(ckv_p), jnp.stack(kr_p), jnp.stack(pool_p),
            jnp.stack(ckv_s), jnp.stack(kr_s), jnp.stack(pool_s))
```

```python
import numpy as np
import concourse.bass as bass
import concourse.mybir as mybir
from concourse.bass_utils import run_bass_kernel_spmd

F32 = mybir.dt.float32
BF16 = mybir.dt.bfloat16
AF = mybir.ActivationFunctionType
ALU = mybir.AluOpType

D = 1024
SEQ = 4096
T = 512
NT_FULL = SEQ // T
NH = 8
QL, KVL, RD, PW = 384, 256, 32, 512
OFF_Q, OFF_KV, OFF_KR, OFF_P, OFF_GA, OFF_GB = 0, 384, 640, 672, 1184, 2208
DFF = 4096
EPS = 1e-6
SM_SCALE = 96 ** -0.5
WINS = (2, 4, 8, 16)
NSEQ_S, TS = 4, 16
TSAMP = NSEQ_S * TS
PAST = 4096
NEG = -30000.0

PIECES = [("in0", 4096), ("in1", 4096), ("in2", 2048), ("uq", 3072)]
PIECES += [("mg%d" % c, 3072) for c in range(8)]
PIECES += [("o0", 4096), ("o1", 4096)]
PIECES += [("up%d" % j, 4096) for j in range(8)]
PIECES += [("dn%d" % j, 4096) for j in range(8)]
POFF = {}
_o = 0
for _n, _s in PIECES:
    POFF[_n] = (_o, _s)
    _o += _s
TOT = _o
NSM = 2560
SLOT = 4096
NV = 27
NCST = 5 + 64


DEBUG_STOP = None


class _Stop(Exception):
    pass


def _stage(name):
    if DEBUG_STOP == name:
        raise _Stop()


class Buf:
    __slots__ = ("name", "w", "r", "excl")

    def __init__(self, name, excl=False):
        self.name = name
        self.w = None
        self.r = []
        self.excl = excl


class Prog:
    ENGS = ("pe", "act", "dve", "pool", "sp")

    def __init__(self):
        self.lists = {e: [] for e in self.ENGS}
        self.nops = {e: 0 for e in self.ENGS}
        self.seen = {e: {} for e in self.ENGS}
        self.dcount = {}
        self.milestones = {e: set() for e in self.ENGS}

    def _collect(self, eng, reads, writes):
        toks = []
        for b in reads:
            if b.w is not None:
                toks.append(b.w)
            if b.excl:
                for t in b.r:
                    if t[0] != eng:
                        toks.append(t)
        same_ok = False
        for b in writes:
            if b.w is not None and (b.w[0] != eng or not same_ok):
                toks.append(b.w)
            for t in b.r:
                if t[0] != eng or not same_ok:
                    toks.append(t)
        waits = {}
        seen = self.seen[eng]
        for k, v in toks:
            if k == "pe" and eng == "pe":
                continue
            if seen.get(k, 0) >= v:
                continue
            if waits.get(k, 0) < v:
                waits[k] = v
        for k, v in waits.items():
            seen[k] = v
            if k in self.milestones:
                self.milestones[k].add(v)
        return list(waits.items())

    def op(self, eng, fn, reads=(), writes=()):
        waits = self._collect(eng, reads, writes)
        self.nops[eng] += 1
        tok = (eng, self.nops[eng])
        self.lists[eng].append((waits, fn, None))
        for b in reads:
            b.r.append(tok)
        for b in writes:
            b.w = tok
            b.r = []
        return tok

    def dma(self, eng, fn, semkey, reads=(), writes=(), batch=False):
        waits = self._collect(eng, reads, writes)
        cnt = self.dcount.get(semkey, 0)
        if not batch and cnt > 0 and self.seen[eng].get(semkey, 0) < cnt:
            waits.append((semkey, cnt))
            self.seen[eng][semkey] = cnt
        cnt += 16
        self.dcount[semkey] = cnt
        tok = (semkey, cnt)
        self.nops[eng] += 1
        self.lists[eng].append((waits, fn, semkey))
        for b in reads:
            b.r.append(tok)
        for b in writes:
            b.w = tok
            b.r = []
        return tok

    @staticmethod
    def handoff(frm, to):
        toks = []
        for b in frm:
            if b.w is not None:
                toks.append(b.w)
            toks += b.r
        for b in to:
            b.r = list(b.r) + toks

    def emit(self, eng, e, sems):
        rank = {}
        for k in self.ENGS:
            ms = sorted(self.milestones[k])
            rank[k] = {v: i + 1 for i, v in enumerate(ms)}
        idx = 0
        for waits, fn, semkey in self.lists[eng]:
            idx += 1
            for k, v in waits:
                if k in rank:
                    e.wait_ge(sems[k], rank[k][v])
                else:
                    e.wait_ge(sems[k], v)
            ins = fn(e)
            if semkey is not None:
                ins.then_inc(sems[semkey], 16)
            elif idx in self.milestones[eng]:
                ins.then_inc(sems[eng], 1)


def build_program(NT=NT_FULL, do_sample=True):
    nc = bass.Bass("TRN2", target_bir_lowering=False)
    P = Prog()

    def din(name, shape, dt=F32):
        return nc.dram_tensor(name, list(shape), dt, kind="ExternalInput").ap()

    def dout(name, shape, dt=F32):
        return nc.dram_tensor(name, list(shape), dt, kind="ExternalOutput").ap()

    xpT = din("xpT", [NT_FULL, 128, 8, T])
    xsT = din("xsT", [128, 8, TSAMP])
    cckvT = din("cckvT", [2, NSEQ_S, 128, 2, PAST])
    ckrT = din("ckrT", [2, NSEQ_S, RD, PAST])
    spT = din("spT", [2, 128, 4, NSEQ_S, 15])
    wfl = din("wfl", [2, 128, TOT])
    wsm = din("wsm", [2, 128, NSM])
    vecs = din("vecs", [128, 2 * NV])
    cst = din("cst", [128, NCST])
    ropeT = din("ropeT", [NT_FULL + 1, 128, 2, T])

    ypT = dout("ypT", [NT_FULL, 128, 8, T])
    ysT = dout("ysT", [128, 8, TSAMP])
    ckvpT = dout("ckvpT", [2, NT_FULL, 128, 2, T])
    krpT = dout("krpT", [2, NT_FULL, RD, T])
    poolpT = dout("poolpT", [2, 128, 4, 15])
    ckvsT = dout("ckvsT", [2, 128, 2, TSAMP])
    krsT = dout("krsT", [2, RD, TSAMP])
    poolsT = dout("poolsT", [2, 128, 4, NSEQ_S, 15])

    wsc = nc.dram_tensor("wsc", [2, 128, TOT], BF16).ap()
    wsmb = nc.dram_tensor("wsmb", [2, 128, NSM], BF16).ap()
    kscr = nc.dram_tensor("kscr", [2, NT_FULL, 4, 96, 2, T], BF16).ap()
    vscr = nc.dram_tensor("vscr", [2, NT_FULL, 4, 128, 4, 2, 65], BF16).ap()
    rkscr = nc.dram_tensor("rkscr", [2, NT_FULL, 128, 4, 8], F32).ap()

    def sb(name, shape, dt):
        return nc.alloc_sbuf_tensor(name, list(shape), dt)

    ones_bf = sb("ones_bf", [128, 128], BF16)
    ones_f = sb("ones_f", [128, 64], F32)
    epsc = sb("epsc", [128, 1], F32)
    m01 = sb("m01", [128, 1], F32)
    sel = sb("sel", [128, 4, 8], BF16)
    vec_sb = sb("vec_sb", [128, 2 * NV], F32)
    cst_sb = sb("cst_sb", [128, NCST], F32)
    gqk = sb("gqk", [128, 2], F32)
    wsm_sb = sb("wsm_sb", [128, 2, NSM], BF16)
    hist = sb("hist", [128, 2, 4, 16], F32)
    xT = sb("xT", [128, 8, T], F32)
    rope = sb("rope", [128, 2, T], F32)
    actbf = sb("actbf", [128, 8, T], BF16)
    NF = 6
    ftmp = [sb("ftmp%d" % i, [128, T], F32) for i in range(NF)]
    sq3 = sb("sq3", [128, 3, T], BF16)
    cqn = sb("cqn", [128, 3, T], BF16)
    ckvb = sb("ckvb", [128, 2, T], BF16)
    ckvf = sb("ckvf", [128, 2, T], F32)
    krr = sb("krr", [128, T], F32)
    ksqr = sb("ksqr", [128, T], BF16)
    pext = sb("pext", [128, 4, 16 + T], F32)
    ptmp = [sb("ptmp%d" % i, [128, 16 + T], F32) for i in range(2)]
    pooled = sb("pooled", [128, 4, T], BF16)
    u_sb = sb("u_sb", [128, 4, T], BF16)
    qf = sb("qf", [128, T], F32)
    qsq = sb("qsq", [128, T], BF16)
    Kown = sb("Kown", [128, 8, T], BF16)
    ksq = [sb("ksq%d" % i, [128, T], BF16) for i in range(4)]
    Vown = sb("Vown", [128, 4, 8, 65], BF16)
    rk_own = sb("rk_own", [128, 4, 8], F32)
    rk_tmp = sb("rk_tmp", [128, 4, 8], F32)
    NKV = 2
    KVE = 4096 + 2080
    big = sb("big", [128, NKV * KVE + 4096], BF16)
    rk_slot = [sb("rk_slot%d" % i, [128, 4, 8], F32) for i in range(NKV)]
    rk_sub = [sb("rk_sub%d" % i, [128, 4, 8], F32) for i in range(4)]
    cstage = [sb("cstage%d" % i, [128, 2, T], F32) for i in range(2)]
    krstage = [sb("krstage%d" % i, [128, T], F32) for i in range(2)]
    NPT = 6
    PT = [sb("PT%d" % i, [128, T], BF16) for i in range(NPT)]
    attn = sb("attn", [128, 4, T], BF16)
    m_sb = sb("m_sb", [128, 8, T], BF16)
    NW = 4
    wslot = [sb("wslot%d" % i, [128, SLOT], BF16) for i in range(NW)]

    def kslot(i):
        return big[:, i * KVE:i * KVE + 4096].rearrange("p (h t) -> p h t", h=8)

    def vslot(i):
        return big[:, i * KVE + 4096:(i + 1) * KVE].rearrange("p (b h d) -> p b h d", b=4, h=8, d=65)

    Qv = big[:, NKV * KVE:NKV * KVE + 4096].rearrange("p (h t) -> p h t", h=8)
    aT = big[:, 0:32 * T].rearrange("p (k t) -> p k t", k=32)

    banks = [nc.alloc_psum_tensor("bank%d" % i, [128, 512], F32) for i in range(8)]

    B = {}

    def buf(name):
        if name not in B:
            B[name] = Buf(name)
        return B[name]

    bank_b = [buf("bank%d" % i) for i in range(8)]
    for b_ in bank_b:
        b_.excl = True
    ftmp_b = [buf("ftmp%d" % i) for i in range(NF)]
    pt_b = [buf("PT%d" % i) for i in range(NPT)]
    wslot_b = [buf("wslot%d" % i) for i in range(NW)]
    kv_b = [buf("kvslot%d" % i) for i in range(NKV)]
    kvs_b = [buf("kvsub%d" % i) for i in range(4)]
    a_b = [buf("aT%d" % i) for i in range(32)]
    x_b = [buf("x%d" % i) for i in range(8)]
    act_b = [buf("act%d" % i) for i in range(8)]
    m_b = [buf("m%d" % i) for i in range(8)]

    state = {"bank": 0, "ft": 0, "pt": 0, "attn_mode": False, "pinned": set(), "nbank": 8}

    def alloc_bank():
        n = state["nbank"]
        while True:
            i = state["bank"] % n
            state["bank"] += 1
            if i not in state["pinned"]:
                return i

    ft_hold = set()

    def alloc_ft():
        while True:
            i = state["ft"] % NF
            state["ft"] += 1
            if i not in ft_hold:
                return i

    def alloc_pt():
        i = state["pt"] % NPT
        state["pt"] += 1
        return i

    def mm(out, lhsT, rhs, start, stop, reads, writes):
        P.op("pe", lambda e, o=out, l=lhsT, r=rhs, s=start, st=stop:
             e.matmul(o, lhsT=l, rhs=r, start=s, stop=st, skip_group_check=True),
             reads=reads, writes=writes)

    def act(out, in_, func, reads, writes, scale=None, bias=None):
        def fn(e, o=out, i=in_, f=func, s=scale, b=bias):
            kw = {}
            if s is not None:
                kw["scale"] = s
            if b is not None:
                kw["bias"] = b
            return e.activation(out=o, in_=i, func=f, **kw)
        P.op("act", fn, reads=reads, writes=writes)

    def tt(eng, out, in0, in1, op, reads, writes):
        P.op(eng, lambda e, o=out, a=in0, b=in1, p=op: e.tensor_tensor(out=o, in0=a, in1=b, op=p),
             reads=reads, writes=writes)

    def stt(eng, out, in0, scalar, in1, op0, op1, reads, writes):
        P.op(eng, lambda e, o=out, a=in0, s=scalar, b=in1, p0=op0, p1=op1:
             e.scalar_tensor_tensor(out=o, in0=a, scalar=s, in1=b, op0=p0, op1=p1),
             reads=reads, writes=writes)

    def ts(eng, out, in0, s1, s2, op0, op1, reads, writes):
        def fn(e, o=out, a=in0, x=s1, y=s2, p0=op0, p1=op1):
            if p1 is None:
                return e.tensor_scalar(out=o, in0=a, scalar1=x, scalar2=None, op0=p0)
            return e.tensor_scalar(out=o, in0=a, scalar1=x, scalar2=y, op0=p0, op1=p1)
        P.op(eng, fn, reads=reads, writes=writes)

    def cp(eng, out, in_, reads, writes):
        P.op(eng, lambda e, o=out, i=in_: e.tensor_copy(out=o, in_=i), reads=reads, writes=writes)

    def copy_any(eng, out, in_, reads, writes):
        if eng == "act":
            act(out, in_, AF.Copy, reads, writes)
        else:
            cp(eng, out, in_, reads, writes)

    def recip(out, in_, reads, writes):
        P.op("dve", lambda e, o=out, i=in_: e.reciprocal(out=o, in_=i), reads=reads, writes=writes)

    def memset(eng, ap, val, writes):
        P.op(eng, lambda e, a=ap, v=val: e.memset(a, v), writes=writes)

    def dma(eng, out, in_, semkey, reads, writes, batch=False):
        return P.dma(eng, lambda e, o=out, i=in_: e.dma_start(out=o, in_=i), semkey,
                     reads=reads, writes=writes, batch=batch)

    cb = buf("consts")
    hist_b = [buf("hist0"), buf("hist1")]
    memset("pool", pext[:], 0.0, [buf("pext")])
    memset("pool", hist[:, 0], 0.0, [hist_b[0]])
    memset("pool", hist[:, 1], 0.0, [hist_b[1]])
    vown_b = buf("Vown")
    memset("pool", Vown[:, :, :, 64:65], 1.0, [vown_b])
    dma("sp", vec_sb[:], vecs, "ld_misc", [], [cb], batch=True)
    dma("sp", cst_sb[:], cst, "ld_misc", [], [cb], batch=True)
    conv_b = {}
    groups = [("A", ["in0", "in1", "in2", "uq"])]
    _bn = ["mg%d" % c for c in range(8)] + ["o0", "o1"]
    groups += [("B%d" % i, _bn[2 * i:2 * i + 2]) for i in range(5)]
    groups += [("C%d" % i, ["up%d" % (2 * i), "up%d" % (2 * i + 1)]) for i in range(4)]
    groups += [("D%d" % i, ["dn%d" % (2 * i), "dn%d" % (2 * i + 1)]) for i in range(4)]
    conv_sched = [1, 5, 4, 4, 1, 5, 4, 4]
    smb = buf("wsmb")
    pending_conv = []

    def issue_conv(l, gname, names):
        key = "cv_%s%d" % (gname, l)
        a = POFF[names[0]][0]
        b_ = POFF[names[-1]][0] + POFF[names[-1]][1]
        gb = conv_b[(gname, l)]
        step = 8192
        o = a
        while o < b_:
            e_ = min(b_, o + step)
            dma("pool", wsc[l, :, o:e_], wfl[l, :, o:e_], key, [], [gb], batch=True)
            o = e_
        gb.w = (key, P.dcount[key])

    dma("pool", wsmb, wsm, "cv_sm", [], [smb], batch=True)
    for l in range(2):
        for gname, names in groups:
            conv_b[(gname, l)] = buf("cv_%s%d" % (gname, l))
            pending_conv.append((l, gname, names))

    def conv_step():
        n = conv_sched.pop(0) if conv_sched else len(pending_conv)
        for _ in range(n):
            if pending_conv:
                issue_conv(*pending_conv.pop(0))

    conv_step()
    piece_group = {}
    for gname, names in groups:
        for n in names:
            piece_group[n] = gname
    smb.w = ("cv_sm", P.dcount["cv_sm"])
    wsm_b = buf("wsm_sb")
    dma("sp", wsm_sb[:], wsmb.rearrange("l p n -> p l n"), "ld_misc", [smb], [wsm_b], batch=True)
    for b_ in (cb, wsm_b):
        b_.w = ("ld_misc", P.dcount["ld_misc"])
    memset("dve", ones_bf[:], 1.0, [cb])
    memset("dve", ones_f[:], 1.0, [cb])
    memset("dve", epsc[:], EPS, [cb])
    memset("dve", m01[0:64, :], 1.0, [cb])
    memset("dve", m01[64:128, :], 0.0, [cb])
    memset("dve", sel[:], 0.0, [cb])
    for p_ in range(4):
        memset("dve", sel[0:64, p_, 2 * p_:2 * p_ + 1], 1.0, [cb])
        memset("dve", sel[64:128, p_, 2 * p_ + 1:2 * p_ + 2], 1.0, [cb])
    for l in range(2):
        tt("dve", gqk[0:96, l:l + 1], vec_sb[0:96, l * NV + 25:l * NV + 26], vec_sb[0:96, l * NV + 26:l * NV + 27],
           ALU.mult, [cb], [buf("gqk")])
    ts("dve", gqk[0:96, :], gqk[0:96, :], SM_SCALE, None, ALU.mult, None, [buf("gqk")], [buf("gqk")])

    def vcol(l, off, c=0):
        return vec_sb[:, l * NV + off + c:l * NV + off + c + 1]
    V_GMIX, V_GMLP, V_GQA, V_GKVA, V_PS = 0, 8, 16, 19, 21
    maskd = cst_sb[:, 0:1]

    def ukvK(l, kc, h):
        return wsm_sb[:, l, kc * 512 + h * 64:kc * 512 + (h + 1) * 64]

    def ukvV(l, kc):
        return wsm_sb[:, l, 1024 + kc * 512:1024 + (kc + 1) * 512]

    def wpool(l, g):
        return wsm_sb[:, l, 2048 + g * 128:2048 + (g + 1) * 128]

    tiles = [("p", t) for t in range(NT)] + ([("s", 0)] if do_sample else [])
    wstream = []
    for kind, t in tiles:
        for l in range(2):
            for n, s in PIECES:
                wstream.append((l, n))
    wst = {"cur": 0, "rec": 0}

    def next_piece(expect):
        idx = wst["cur"]
        while wst["rec"] < min(len(wstream), idx + NW):
            j = wst["rec"]
            l, n = wstream[j]
            o, s = POFF[n]
            si = j % NW
            dma("sp", wslot[si][:, 0:s], wsc[l, :, o:o + s], "wslot%d" % si,
                [conv_b[(piece_group[n], l)]], [wslot_b[si]])
            wst["rec"] += 1
        assert wstream[idx][1] == expect, (wstream[idx], expect)
        wst["cur"] += 1
        return wslot[idx % NW], wslot_b[idx % NW]

    def rmsnorm_x(l, gcol, Tn):
        for c in range(8):
            act(actbf[:, c, :Tn], xT[:, c, :Tn], AF.Square, [x_b[c]], [act_b[c]])
        bi = alloc_bank()
        for c in range(8):
            mm(banks[bi][:, :Tn], ones_bf[:, :], actbf[:, c, :Tn], c == 0, c == 7, [act_b[c], cb], [bank_b[bi]])
        f1 = alloc_ft()
        act(ftmp[f1][:, :Tn], banks[bi][:, :Tn], AF.Ln, [bank_b[bi], cb], [ftmp_b[f1]], scale=1.0 / D, bias=epsc[:, 0:1])
        f2 = alloc_ft()
        act(ftmp[f2][:, :Tn], ftmp[f1][:, :Tn], AF.Exp, [ftmp_b[f1]], [ftmp_b[f2]], scale=-0.5)
        for c in range(8):
            stt("dve", actbf[:, c, :Tn], xT[:, c, :Tn], vcol(l, gcol, c), ftmp[f2][:, :Tn], ALU.mult, ALU.mult,
                [x_b[c], ftmp_b[f2], cb], [act_b[c]])

    def kv_generate(l, src_bf, src_b, kr_f, kr_b, nk, Kd, Vd, rkd, kd_b, extra_w=(), set_ones=False, bcast_rope=True):
        nkb = (nk + 127) // 128
        kw = nk if nk < 128 else 128
        if bcast_rope:
            act(Kd[64:96, :, :nk], kr_f[64:96, :nk].unsqueeze(1).broadcast_to([32, 8, nk]), AF.Copy, [kr_b], [kd_b])
        else:
            cp("dve", Kd[64:96, 0, :nk], kr_f[64:96, :nk], [kr_b], [kd_b])
        ksr_b = buf("ksqr")
        tt("pool", ksqr[64:96, :nk], kr_f[64:96, :nk], kr_f[64:96, :nk], ALU.mult, [kr_b], [ksr_b])
        _stage("kvA")
        rb = alloc_bank()
        state["pinned"].add(rb)
        psr = banks[rb][:, 0:32].rearrange("p (b h) -> p b h", b=4)
        memset("dve", banks[rb][:, 0:32], 0.0, [bank_b[rb]])
        if set_ones:
            memset("pool", Vd[:, :, :, 64:65], 1.0, [kd_b])
        kbanks = []
        for hp in range(4):
            bi = alloc_bank()
            kbanks.append(bi)
            for kc in range(2):
                mm(banks[bi][:, :nk], wsm_sb[:, l, kc * 512 + hp * 128:kc * 512 + (hp + 1) * 128], src_bf[:, kc, :nk],
                   kc == 0, kc == 1, [src_b, wsm_b], [bank_b[bi]])
        for hp in range(4):
            bi = kbanks[hp]
            cp("dve", Kd[0:64, 2 * hp + 1, :nk], banks[bi][64:128, :nk], [bank_b[bi]], [kd_b])
            act(Kd[0:64, 2 * hp, :nk], banks[bi][0:64, :nk], AF.Copy, [bank_b[bi]], [kd_b])
            act(ksq[hp][:, :nk], banks[bi][:, :nk], AF.Square, [bank_b[bi]], [buf("ksq%d" % hp)])
        for kb in range(nkb):
            bi = alloc_bank()
            for kc in range(2):
                mm(banks[bi][0:kw, :], src_bf[:, kc, kb * 128:kb * 128 + kw], ukvV(l, kc), kc == 0, kc == 1,
                   [src_b, wsm_b], [bank_b[bi]])
            copy_any("act" if kb % 2 else "dve", Vd[0:kw, kb, :, 0:64], banks[bi][0:kw, :].rearrange("p (h d) -> p h d", h=8),
                     [bank_b[bi]], [kd_b] + list(extra_w))
        for hp in range(4):
            for kb in range(nkb):
                mm(psr[0:kw, kb, :], ksq[hp][:, kb * 128:kb * 128 + kw], sel[:, hp, :], False, False,
                   [buf("ksq%d" % hp), cb], [bank_b[rb]])
                if hp == 0:
                    mm(psr[0:kw, kb, :], ksqr[64:96, kb * 128:kb * 128 + kw], ones_bf[64:96, 0:8], False, False,
                       [ksr_b, cb], [bank_b[rb]])
        rtb = buf("rk_tmp")
        act(rk_tmp[0:kw, 0:nkb, :], psr[0:kw, 0:nkb, :], AF.Ln, [bank_b[rb], cb], [rtb], scale=1.0 / 96, bias=epsc[0:kw, 0:1])
        act(rkd[0:kw, 0:nkb, :], rk_tmp[0:kw, 0:nkb, :], AF.Exp, [rtb], [kd_b], scale=-0.5)
        state["pinned"].discard(rb)
        _stage("kvE")

    def tile_layer(kind, t, l, first_kv_handoff):
        samp = kind == "s"
        Tn = TSAMP if samp else T
        nseg, seg = (NSEQ_S, TS) if samp else (1, T)
        ri = NT_FULL if samp else t

        if first_kv_handoff:
            Prog.handoff(a_b, kv_b + kvs_b + [buf("Q")])
            if samp and l == 0:
                Prog.handoff(kvs_b, kv_b)

        if l == 0:
            for c in range(8):
                pass
            src = xsT if samp else xpT[t]
            for c in range(8):
                dma("sp", xT[:, c, :Tn], src[:, c, :Tn], "ld_x", [], [x_b[c]])
            dma("sp", rope[:, :, :Tn], ropeT[ri][:, :, :Tn], "ld_rope", [], [buf("rope")])
        rope_b = buf("rope")
        cosT = rope[:, 0, :Tn]
        sinT = rope[:, 1, :Tn]

        _stage("setup")
        rmsnorm_x(l, V_GMIX, Tn)

        _stage("norm1")
        ws, wb = next_piece("in0")
        w3 = ws[:, 0:4096].rearrange("p (k c) -> p k c", k=8)
        cqbanks = []
        for c in range(3):
            bi = alloc_bank()
            cqbanks.append(bi)
            for kc in range(8):
                mm(banks[bi][:, :Tn], w3[:, kc, c * 128:(c + 1) * 128], actbf[:, kc, :Tn], kc == 0, kc == 7,
                   [wb, act_b[kc]], [bank_b[bi]])
            act(sq3[:, c, :Tn], banks[bi][:, :Tn], AF.Square, [bank_b[bi]], [buf("sq3")])
        nb = alloc_bank()
        for c in range(3):
            mm(banks[nb][:, :Tn], ones_bf[:, :], sq3[:, c, :Tn], c == 0, c == 2, [buf("sq3"), cb], [bank_b[nb]])
        f1 = alloc_ft()
        act(ftmp[f1][:, :Tn], banks[nb][:, :Tn], AF.Ln, [bank_b[nb], cb], [ftmp_b[f1]], scale=1.0 / QL, bias=epsc[:, 0:1])
        f2 = alloc_ft()
        act(ftmp[f2][:, :Tn], ftmp[f1][:, :Tn], AF.Exp, [ftmp_b[f1]], [ftmp_b[f2]], scale=-0.5)
        for c in range(3):
            bi = cqbanks[c]
            stt("dve", cqn[:, c, :Tn], banks[bi][:, :Tn], vcol(l, V_GQA, c), ftmp[f2][:, :Tn], ALU.mult, ALU.mult,
                [bank_b[bi], ftmp_b[f2], cb], [buf("cqn")])
        bi = alloc_bank()
        for kc in range(8):
            mm(banks[bi][:, :Tn], w3[:, kc, 384:512], actbf[:, kc, :Tn], kc == 0, kc == 7, [wb, act_b[kc]], [bank_b[bi]])
        f1 = alloc_ft()
        f2 = alloc_ft()
        tt("dve", ftmp[f1][64:96, :Tn], banks[bi][0:32, :Tn], sinT[0:32], ALU.mult, [bank_b[bi], rope_b], [ftmp_b[f1]])
        tt("dve", ftmp[f2][64:96, :Tn], banks[bi][64:96, :Tn], cosT[64:96], ALU.mult, [bank_b[bi], rope_b], [ftmp_b[f2]])
        krr_b = buf("krr")
        tt("dve", krr[64:96, :Tn], ftmp[f1][64:96, :Tn], ftmp[f2][64:96, :Tn], ALU.add, [ftmp_b[f1], ftmp_b[f2]], [krr_b])
        if samp:
            dma("pool", krsT[l], krr[64:96, :Tn], "st_kr", [krr_b], [])
        else:
            dma("pool", krpT[l, t], krr[64:96, :Tn], "st_kr", [krr_b], [])
        conv_step()
        ws, wb = next_piece("in1")
        w3 = ws[:, 0:4096].rearrange("p (k c) -> p k c", k=8)
        kvbanks = []
        for c in range(2):
            bi = alloc_bank()
            kvbanks.append(bi)
            for kc in range(8):
                mm(banks[bi][:, :Tn], w3[:, kc, c * 128:(c + 1) * 128], actbf[:, kc, :Tn], kc == 0, kc == 7,
                   [wb, act_b[kc]], [bank_b[bi]])
            act(sq3[:, c, :Tn], banks[bi][:, :Tn], AF.Square, [bank_b[bi]], [buf("sq3")])
        nb = alloc_bank()
        for c in range(2):
            mm(banks[nb][:, :Tn], ones_bf[:, :], sq3[:, c, :Tn], c == 0, c == 1, [buf("sq3"), cb], [bank_b[nb]])
        f1 = alloc_ft()
        act(ftmp[f1][:, :Tn], banks[nb][:, :Tn], AF.Ln, [bank_b[nb], cb], [ftmp_b[f1]], scale=1.0 / KVL, bias=epsc[:, 0:1])
        f2 = alloc_ft()
        act(ftmp[f2][:, :Tn], ftmp[f1][:, :Tn], AF.Exp, [ftmp_b[f1]], [ftmp_b[f2]], scale=-0.5)
        ckvf_b, ckvb_b = buf("ckvf"), buf("ckvb")
        for c in range(2):
            bi = kvbanks[c]
            stt("dve", ckvf[:, c, :Tn], banks[bi][:, :Tn], vcol(l, V_GKVA, c), ftmp[f2][:, :Tn], ALU.mult, ALU.mult,
                [bank_b[bi], ftmp_b[f2], cb], [ckvf_b])
        for c in range(2):
            bi = kvbanks[c]
            stt("dve", ckvb[:, c, :Tn], banks[bi][:, :Tn], vcol(l, V_GKVA, c), ftmp[f2][:, :Tn], ALU.mult, ALU.mult,
                [bank_b[bi], ftmp_b[f2], cb], [ckvb_b])
        if samp:
            dma("pool", ckvsT[l], ckvf[:, :, :Tn], "st_ckv", [ckvf_b], [])
        else:
            dma("pool", ckvpT[l, t], ckvf[:, :, :Tn], "st_ckv", [ckvf_b], [])
        pext_b = buf("pext")
        pv = pext[:, :, 0:nseg * (16 + seg)].rearrange("p g (s j) -> p g s j", s=nseg)
        if samp:
            for g in range(4):
                dma("sp", pv[:, g, :, 1:16], spT[l, :, g], "ld_hist", [], [pext_b], batch=(g > 0))
            pext_b.w = ("ld_hist", P.dcount["ld_hist"])
        else:
            cp("pool", pv[:, :, 0, 0:16], hist[:, l], [hist_b[l]], [pext_b])
        for g in range(4):
            if g == 2:
                ws, wb = next_piece("in2")
                w3 = ws[:, 0:2048].rearrange("p (k c) -> p k c", k=8)
            col = (256 + g * 128) if g < 2 else (g - 2) * 128
            bi = alloc_bank()
            for kc in range(8):
                mm(banks[bi][:, :Tn], w3[:, kc, col:col + 128], actbf[:, kc, :Tn], kc == 0, kc == 7,
                   [wb, act_b[kc]], [bank_b[bi]])
            act(pv[:, g, :, 16:16 + seg], banks[bi][:, :Tn].rearrange("p (s j) -> p s j", s=nseg), AF.Copy,
                [bank_b[bi]], [pext_b])
        if not samp:
            cp("pool", hist[:, l], pv[:, :, 0, seg:seg + 16], [pext_b], [hist_b[l]])
            if t == NT_FULL - 1:
                dma("pool", poolpT[l], pv[:, :, 0, 16 + seg - 15:16 + seg], "st_pool", [pext_b], [])
        else:
            for g in range(4):
                dma("pool", poolsT[l, :, g], pv[:, g, :, 17:32], "st_pool", [pext_b], [], batch=(g > 0))

        _stage("pool")
        conv_step()
        ws, wb = next_piece("uq")
        wq = ws[:, 0:3072].rearrange("p (k h c) -> p k h c", k=3, h=8)
        q_b = buf("Q")
        krr_q = krstage[1]
        qsq_t = [qsq, ksqr]
        qsq_bs = [buf("qsq"), buf("ksqr")]
        qf_t = [qf, krr_q]
        qf_bs = [buf("qf"), buf("cstage1")]
        for hp in range(4):
            hb = []
            for k2 in range(2):
                h = 2 * hp + k2
                bi = alloc_bank()
                hb.append(bi)
                for kc in range(3):
                    mm(banks[bi][:, :Tn], wq[:, kc, h, :], cqn[:, kc, :Tn], kc == 0, kc == 2, [wb, buf("cqn")], [bank_b[bi]])
                act(qsq_t[k2][0:96, :Tn], banks[bi][0:96, :Tn], AF.Square, [bank_b[bi]], [qsq_bs[k2]])
                f1 = alloc_ft()
                f2 = alloc_ft()
                tt("dve", ftmp[f1][64:96, :Tn], banks[bi][96:128, :Tn], sinT[96:128], ALU.mult, [bank_b[bi], rope_b], [ftmp_b[f1]])
                tt("dve", ftmp[f2][64:96, :Tn], banks[bi][64:96, :Tn], cosT[64:96], ALU.mult, [bank_b[bi], rope_b], [ftmp_b[f2]])
                tt("dve", qf_t[k2][64:96, :Tn], ftmp[f1][64:96, :Tn], ftmp[f2][64:96, :Tn], ALU.add,
                   [ftmp_b[f1], ftmp_b[f2]], [qf_bs[k2]])
            rs = []
            for k2 in range(2):
                nb = alloc_bank()
                mm(banks[nb][0:96, :Tn], ones_bf[0:96, 0:96], qsq_t[k2][0:96, :Tn], True, True, [qsq_bs[k2], cb], [bank_b[nb]])
                f3 = alloc_ft()
                act(ftmp[f3][0:96, :Tn], banks[nb][0:96, :Tn], AF.Ln, [bank_b[nb], cb], [ftmp_b[f3]], scale=1.0 / 96,
                    bias=epsc[0:96, 0:1])
                f4 = alloc_ft()
                ft_hold.add(f4)
                act(ftmp[f4][0:96, :Tn], ftmp[f3][0:96, :Tn], AF.Exp, [ftmp_b[f3]], [ftmp_b[f4]], scale=-0.5)
                rs.append(f4)
            for k2 in range(2):
                h = 2 * hp + k2
                bi, f4 = hb[k2], rs[k2]
                stt("dve", Qv[0:64, h, :Tn], banks[bi][0:64, :Tn], gqk[0:64, l:l + 1], ftmp[f4][0:64, :Tn], ALU.mult, ALU.mult,
                    [bank_b[bi], ftmp_b[f4], buf("gqk")], [q_b])
                stt("dve", Qv[64:96, h, :Tn], qf_t[k2][64:96, :Tn], gqk[64:96, l:l + 1], ftmp[f4][64:96, :Tn], ALU.mult, ALU.mult,
                    [qf_bs[k2], ftmp_b[f4], buf("gqk")], [q_b])
                ft_hold.discard(f4)

        _stage("q")
        conv_step()
        kown_b = buf("Kown")
        kv_generate(l, ckvb, ckvb_b, krr, krr_b, Tn, Kown, Vown, rk_own, kown_b, extra_w=[vown_b])
        vown_b.w = kown_b.w
        if (not samp) and t < NT - 1:
            sk = "st_kv%d" % l
            scr_b = [buf("scr_%d_%d_%d" % (l, t, hg)) for hg in range(4)]
            for hg in range(4):
                dma("pool", kscr[l, t, hg], Kown[0:96, hg * 2:(hg + 1) * 2, :], sk, [kown_b], [scr_b[hg]], batch=(hg > 0))
                dma("pool", vscr[l, t, hg], Vown[:, :, hg * 2:(hg + 1) * 2, :], sk, [kown_b, vown_b], [scr_b[hg]], batch=True)
            dma("pool", rkscr[l, t], rk_own[:], sk, [kown_b], scr_b, batch=True)
            for b_ in scr_b:
                b_.w = (sk, P.dcount[sk])

        _stage("win")
        W_ = 16 + seg
        pooled_b = buf("pooled")
        ptb = [buf("ptmp0"), buf("ptmp1")]
        for g, w in enumerate(WINS):
            cur, cur_b = pv[:, g], pext_b
            sh, k = 1, 0
            while sh < w:
                lo = 2 * sh - 1
                dst = ptmp[k][:, 0:nseg * W_].rearrange("p (s j) -> p s j", s=nseg)
                tt("pool", dst[:, :, lo:W_], cur[:, :, lo:W_], cur[:, :, lo - sh:W_ - sh], ALU.add, [cur_b], [ptb[k]])
                cur, cur_b = dst, ptb[k]
                k ^= 1
                sh *= 2
            stt("dve", pooled[:, g, :Tn].rearrange("p (s j) -> p s j", s=nseg), cur[:, :, 16:W_], 1.0 / w,
                pv[:, g, :, 16:W_], ALU.mult, ALU.subtract, [cur_b, pext_b], [pooled_b])
            if (not samp) and t == 0:
                f1 = alloc_ft()
                tt("pool", ftmp[f1][:, 0:16], cur[:, 0, 16:32], cst_sb[:, 5 + g * 16:5 + (g + 1) * 16], ALU.mult,
                   [cur_b, cb], [ftmp_b[f1]])
                tt("pool", pooled[:, g, 0:16], ftmp[f1][:, 0:16], pv[:, g, 0, 16:32], ALU.subtract,
                   [ftmp_b[f1], pext_b], [pooled_b])
        _stage("kv")
        attn_b = buf("attn")
        state["attn_mode"] = True
        state["bank"] = 0
        state["nbank"] = 6 if samp else 5

        def fin_a(ai, ncols):
            f2 = alloc_ft()
            ft_hold.add(f2)
            cp("dve", ftmp[f2][0:65, :ncols], banks[ai][0:65, :ncols], [bank_b[ai]], [ftmp_b[f2]])
            return f2

        def fin_b(f2, ncols, out_ap, out_rearr, fbank=None):
            f1 = alloc_ft()
            recip(ftmp[f1][64:65, :ncols], ftmp[f2][64:65, :ncols], [ftmp_b[f2]], [ftmp_b[f1]])
            bi = alloc_bank() if fbank is None else fbank
            mm(banks[bi][0:64, :ncols], ones_f[64:65, 0:64], ftmp[f1][64:65, :ncols], True, True, [ftmp_b[f1], cb], [bank_b[bi]])
            i0 = ftmp[f2][0:64, :ncols]
            i1 = banks[bi][0:64, :ncols]
            if out_rearr:
                i0 = i0.rearrange("p (hp k q) -> p hp k q", hp=4, k=2)
                i1 = i1.rearrange("p (hp k q) -> p hp k q", hp=4, k=2)
                for k_ in range(2):
                    tt("dve", attn[k_ * 64:(k_ + 1) * 64, :, out_ap], i0[:, :, k_, :], i1[:, :, k_, :], ALU.mult,
                       [ftmp_b[f2], bank_b[bi]], [attn_b])
            else:
                h_ = out_ap
                tt("dve", attn[(h_ % 2) * 64:(h_ % 2) * 64 + 64, h_ // 2, :ncols], i0, i1, ALU.mult,
                   [ftmp_b[f2], bank_b[bi]], [attn_b])
            ft_hold.discard(f2)

        def finalize(ai, ncols, out_ap, out_rearr):
            fin_b(fin_a(ai, ncols), ncols, out_ap, out_rearr)

        if not samp:
            kvst = {"cur": 0, "rec": 0}
            kvlist = [(hg, j) for hg in range(4) for j in range(t)]

            def kv_acquire():
                idx = kvst["cur"]
                while kvst["rec"] < min(len(kvlist), idx + 3):
                    jj = kvst["rec"]
                    hg_, j_ = kvlist[jj]
                    ss = jj % 4
                    sb_ = buf("scr_%d_%d_%d" % (l, j_, hg_))
                    key = "kvsub%d" % ss
                    ho = (ss % 2) * 2
                    dma("sp", kslot(ss // 2)[0:96, ho:ho + 2, :], kscr[l, j_, hg_], key, [sb_], [kvs_b[ss]])
                    dma("sp", vslot(ss // 2)[:, :, ho:ho + 2, :], vscr[l, j_, hg_], key, [sb_], [kvs_b[ss]], batch=True)
                    dma("sp", rk_sub[ss][:], rkscr[l, j_], key, [sb_], [kvs_b[ss]], batch=True)
                    kvs_b[ss].w = (key, P.dcount[key])
                    kvst["rec"] += 1
                kvst["cur"] += 1
                return idx % 4

            blocks = [(hg, j, hh, kb) for hg in range(4) for j in range(t + 1) for hh in range(2) for kb in range(4)]
            srcs = {}

            def get_src(hg, j):
                if (hg, j) not in srcs:
                    if j != t:
                        ss = kv_acquire()
                        srcs[(hg, j)] = (kslot(ss // 2), vslot(ss // 2), rk_sub[ss], kvs_b[ss], (ss % 2) * 2, False)
                    else:
                        srcs[(hg, j)] = (Kown, Vown, rk_own, kown_b, hg * 2, True)
                return srcs[(hg, j)]

            def emit_S(b):
                hg, j, hh, kb = b
                Ks, Vs, rks, src_b, hoff, own = get_src(hg, j)
                h = hg * 2 + hh
                q0 = kb * 128 if own else 0
                N = T - q0
                bi = alloc_bank()
                mm(banks[bi][:, :N], Ks[0:96, hoff + hh, kb * 128:(kb + 1) * 128], Qv[0:96, h, q0:T], True, True,
                   [src_b, q_b], [bank_b[bi]])
                return bi, q0, N

            def emit_rest(b, info):
                hg, j, hh, kb = b
                bi, q0, N = info
                Ks, Vs, rks, src_b, hoff, own = get_src(hg, j)
                h = hg * 2 + hh
                ai = 6 + hh
                pi = alloc_pt()
                sc = rks[:, kb, h:h + 1]
                act(PT[pi][:, :N], banks[bi][:, :N], AF.Exp, [bank_b[bi], src_b], [pt_b[pi]], scale=sc)
                if own:
                    mm(banks[ai][0:65, q0:q0 + 64], Vs[0:64, kb, hoff + hh, :], PT[pi][0:64, 0:64], (j == 0 and kb == 0),
                       False, [pt_b[pi], src_b, vown_b], [bank_b[ai]])
                    mm(banks[ai][0:65, q0 + 64:T], Vs[:, kb, hoff + hh, :], PT[pi][:, 64:N], False,
                       kb == 3, [pt_b[pi], src_b, vown_b], [bank_b[ai]])
                else:
                    mm(banks[ai][0:65, q0:T], Vs[:, kb, hoff + hh, :], PT[pi][:, :N], (j == 0 and kb == 0),
                       False, [pt_b[pi], src_b], [bank_b[ai]])

            LOOK = 4
            infos = {}
            pend_fin = []
            for i in range(min(LOOK, len(blocks))):
                infos[i] = emit_S(blocks[i])
            for i, b in enumerate(blocks):
                while pend_fin and pend_fin[0][2] <= i:
                    f2_, h_, _ = pend_fin.pop(0)
                    fin_b(f2_, T, h_, False, fbank=5)
                emit_rest(b, infos.pop(i))
                hg, j, hh, kb = b
                if j == t and kb == 3:
                    pend_fin.append((fin_a(6 + hh, T), hg * 2 + hh, i + 6))
                if i + LOOK < len(blocks):
                    infos[i + LOOK] = emit_S(blocks[i + LOOK])
            for f2_, h_, _ in pend_fin:
                fin_b(f2_, T, h_, False, fbank=5)
        else:
            cst_ = {"cur": 0, "rec": 0}
            clist = [(s, g) for s in range(NSEQ_S) for g in range(PAST // T)]
            cs_b = [buf("cstage0"), buf("cstage1")]

            def c_acquire():
                idx = cst_["cur"]
                while cst_["rec"] < min(len(clist), idx + 2):
                    jj = cst_["rec"]
                    s_, g_ = clist[jj]
                    si = jj % 2
                    dma("sp", cstage[si][:], cckvT[l, s_, :, :, g_ * T:(g_ + 1) * T], "cstage%d" % si, [], [cs_b[si]])
                    dma("sp", krstage[si][64:96, :], ckrT[l, s_, :, g_ * T:(g_ + 1) * T], "cstage%d" % si, [], [cs_b[si]], batch=True)
                    cs_b[si].w = ("cstage%d" % si, P.dcount["cstage%d" % si])
                    cst_["rec"] += 1
                cst_["cur"] += 1
                return idx % 2

            stg = [(sq3, buf("sq3")), (cqn, buf("cqn"))]
            ngrp = PAST // T
            items = [(s_, g_) for s_ in range(NSEQ_S) for g_ in range(ngrp + 1)]
            gen = {}

            def gen_kv(it):
                s_, g_ = it
                if g_ == ngrp:
                    gen[it] = (Kown, Vown, rk_own, kown_b, 1, TSAMP, True)
                    return
                ci = c_acquire()
                n_ = s_ * ngrp + g_
                si = n_ % NKV
                stt_, ccb_b = stg[n_ % 2]
                cbf = stt_[:, 0:2, :]
                act(cbf, cstage[ci][:], AF.Copy, [cs_b[ci]], [ccb_b])
                kv_generate(l, cbf, ccb_b, krstage[ci], cs_b[ci], T, kslot(si), vslot(si), rk_slot[si], kv_b[si],
                            set_ones=True, bcast_rope=False)
                gen[it] = (kslot(si), vslot(si), rk_slot[si], kv_b[si], 4, 128, False)

            def do_S(it):
                s_, g_ = it
                Ks, Vs, rks, src_b, nkb, kw, own = gen[it]
                bi = alloc_bank()
                for kb in range(nkb):
                    mm(banks[bi][0:kw, kb * 128:(kb + 1) * 128].rearrange("p (h q) -> p h q", h=8),
                       Ks[64:96, 0, kb * 128:kb * 128 + kw], Qv[64:96, :, s_ * 16:(s_ + 1) * 16], True, False,
                       [src_b, q_b], [bank_b[bi]])
                    for h in range(NH):
                        c0 = kb * 128 + h * 16
                        mm(banks[bi][0:kw, c0:c0 + 16], Ks[0:64, h, kb * 128:kb * 128 + kw], Qv[0:64, h, s_ * 16:(s_ + 1) * 16],
                           False, h == NH - 1, [src_b, q_b], [bank_b[bi]])
                f1 = alloc_ft()
                nc_ = nkb * 128
                tt("dve", ftmp[f1][0:kw, 0:nc_].rearrange("p (b h q) -> p b h q", b=nkb, h=8),
                   banks[bi][0:kw, 0:nc_].rearrange("p (b h q) -> p b h q", b=nkb, h=8),
                   rks[0:kw, 0:nkb, :].unsqueeze(3).broadcast_to([kw, nkb, 8, 16]), ALU.mult,
                   [bank_b[bi], src_b], [ftmp_b[f1]])
                pi = alloc_pt()
                if own:
                    act(PT[pi][0:kw, 0:nc_], ftmp[f1][0:kw, 0:nc_], AF.Exp, [ftmp_b[f1], cb], [pt_b[pi]],
                        bias=cst_sb[0:kw, 1 + s_:2 + s_])
                else:
                    act(PT[pi][0:kw, 0:nc_], ftmp[f1][0:kw, 0:nc_], AF.Exp, [ftmp_b[f1]], [pt_b[pi]])
                return pi

            def do_PV(it, pi):
                s_, g_ = it
                Ks, Vs, rks, src_b, nkb, kw, own = gen[it]
                ai = 6 + (s_ % 2)
                if g_ == 0:
                    memset("dve", banks[ai][0:65, 0:128], 0.0, [bank_b[ai]])
                for kb in range(nkb):
                    for h in range(NH):
                        c0 = kb * 128 + h * 16
                        mm(banks[ai][0:65, h * 16:(h + 1) * 16], Vs[0:kw, kb, h, :], PT[pi][0:kw, c0:c0 + 16], False, False,
                           [pt_b[pi], src_b] + ([vown_b] if own else []), [bank_b[ai]])
                if g_ == ngrp:
                    finalize(ai, 128, slice(s_ * 16, (s_ + 1) * 16), True)

            gen_kv(items[0])
            for i, it in enumerate(items):
                pi = do_S(it)
                if i + 1 < len(items):
                    gen_kv(items[i + 1])
                do_PV(it, pi)
        state["attn_mode"] = False
        state["nbank"] = 8

        _stage("attn")
        u_b = buf("u")
        for g in range(4):
            bi = alloc_bank()
            mm(banks[bi][:, :Tn], wpool(l, g), pooled[:, g, :Tn], True, True, [pooled_b, wsm_b], [bank_b[bi]])
            act(u_sb[:, g, :Tn], banks[bi][:, :Tn], AF.Identity, [bank_b[bi], cb], [u_b], scale=vcol(l, V_PS, g))

        for c in range(8):
            ws, wb = next_piece("mg%d" % c)
            wao = ws[:, 0:512].rearrange("p (h c) -> p h c", h=4)
            wpo = ws[:, 512:1024].rearrange("p (k c) -> p k c", k=4)
            wga = ws[:, 1024:2048].rearrange("p (k c) -> p k c", k=8)
            wgb = ws[:, 2048:3072].rearrange("p (k c) -> p k c", k=8)
            ba = alloc_bank()
            for hp in range(4):
                mm(banks[ba][:, :Tn], wao[:, hp, :], attn[:, hp, :Tn], hp == 0, hp == 3, [wb, attn_b], [bank_b[ba]])
            bb = alloc_bank()
            for kc in range(4):
                mm(banks[bb][:, :Tn], wpo[:, kc, :], u_sb[:, kc, :Tn], kc == 0, kc == 3, [wb, u_b], [bank_b[bb]])
            bga = alloc_bank()
            for kc in range(8):
                mm(banks[bga][:, :Tn], wga[:, kc, :], actbf[:, kc, :Tn], kc == 0, kc == 7, [wb, act_b[kc]], [bank_b[bga]])
            bgb = alloc_bank()
            for kc in range(8):
                mm(banks[bgb][:, :Tn], wgb[:, kc, :], actbf[:, kc, :Tn], kc == 0, kc == 7, [wb, act_b[kc]], [bank_b[bgb]])
            fa, fb = alloc_ft(), alloc_ft()
            act(ftmp[fa][:, :Tn], banks[bga][:, :Tn], AF.Tanh, [bank_b[bga]], [ftmp_b[fa]], scale=0.5)
            act(ftmp[fb][:, :Tn], banks[bgb][:, :Tn], AF.Tanh, [bank_b[bgb]], [ftmp_b[fb]], scale=0.5)
            stt("dve", ftmp[fa][:, :Tn], ftmp[fa][:, :Tn], 1.0, banks[ba][:, :Tn], ALU.add, ALU.mult,
                [ftmp_b[fa], bank_b[ba]], [ftmp_b[fa]])
            stt("dve", ftmp[fb][:, :Tn], ftmp[fb][:, :Tn], 1.0, banks[bb][:, :Tn], ALU.add, ALU.mult,
                [ftmp_b[fb], bank_b[bb]], [ftmp_b[fb]])
            tt("pool", m_sb[:, c, :Tn], ftmp[fa][:, :Tn], ftmp[fb][:, :Tn], ALU.add, [ftmp_b[fa], ftmp_b[fb]], [m_b[c]])

        _stage("merge")
        for j in range(2):
            ws, wb = next_piece("o%d" % j)
            w3 = ws[:, 0:4096].rearrange("p (k c) -> p k c", k=8)
            for cc in range(4):
                oc = j * 4 + cc
                bi = alloc_bank()
                for kc in range(8):
                    mm(banks[bi][:, :Tn], w3[:, kc, cc * 128:(cc + 1) * 128], m_sb[:, kc, :Tn], kc == 0, kc == 7,
                       [wb, m_b[kc]], [bank_b[bi]])
                stt("dve", xT[:, oc, :Tn], banks[bi][:, :Tn], 0.5, xT[:, oc, :Tn], ALU.mult, ALU.add,
                    [bank_b[bi], x_b[oc]], [x_b[oc]])

        _stage("wo")
        conv_step()
        rmsnorm_x(l, V_GMLP, Tn)
        conv_step()
        Prog.handoff(kv_b + kvs_b + [buf("Q")], a_b)
        for j in range(8):
            ws, wb = next_piece("up%d" % j)
            w3 = ws[:, 0:4096].rearrange("p (k c) -> p k c", k=8)
            for cc in range(4):
                oc = j * 4 + cc
                bi = alloc_bank()
                for kc in range(8):
                    mm(banks[bi][:, :Tn], w3[:, kc, cc * 128:(cc + 1) * 128], actbf[:, kc, :Tn], kc == 0, kc == 7,
                       [wb, act_b[kc]], [bank_b[bi]])
                f1 = alloc_ft()
                act(ftmp[f1][:, :Tn], banks[bi][:, :Tn], AF.Relu, [bank_b[bi]], [ftmp_b[f1]])
                tt("pool" if oc % 2 else "dve", aT[:, oc, :Tn], ftmp[f1][:, :Tn], ftmp[f1][:, :Tn], ALU.mult,
                   [ftmp_b[f1]], [a_b[oc]])
        for oc in range(8):
            ws, wb = next_piece("dn%d" % oc)
            w3 = ws[:, 0:4096].rearrange("p (k c) -> p k c", k=32)
            bi = alloc_bank()
            for kc in range(32):
                mm(banks[bi][:, :Tn], w3[:, kc, :], aT[:, kc, :Tn], kc == 0, kc == 31, [wb, a_b[kc]], [bank_b[bi]])
            tt("dve", xT[:, oc, :Tn], banks[bi][:, :Tn], xT[:, oc, :Tn], ALU.add, [bank_b[bi], x_b[oc]], [x_b[oc]])
            if l == 1:
                dst = ysT if samp else ypT[t]
                dma("pool", dst[:, oc, :Tn], xT[:, oc, :Tn], "st_y", [x_b[oc]], [])

    first = True
    try:
        for kind, t in tiles:
            for l in range(2):
                tile_layer(kind, t, l, not first)
                first = False
    except _Stop:
        pass

    store_keys = [k for k in P.dcount if k.startswith("st_")]
    fin = [(k, P.dcount[k]) for k in store_keys]
    P.lists["sp"].append((fin, None, None))

    semkeys = list(Prog.ENGS) + sorted(P.dcount.keys())
    import contextlib
    with contextlib.ExitStack() as es:
        sems = {k: es.enter_context(nc.semaphore("s_" + k)) for k in semkeys}
        block = es.enter_context(nc.Block())

        def mk(eng):
            def body(e):
                lst = P.lists[eng]
                if eng == "sp" and lst and lst[-1][1] is None:
                    fin_ = lst.pop()
                    P.emit(eng, e, sems)
                    for k, v in fin_[0]:
                        e.wait_ge(sems[k], v)
                else:
                    P.emit(eng, e, sems)
            return body
        block.tensor(mk("pe"))
        block.scalar(mk("act"))
        block.vector(mk("dve"))
        block.gpsimd(mk("pool"))
        block.sync(mk("sp"))
    return nc, P


def _lay_kc(W):
    K, C = W.shape
    return np.ascontiguousarray(W.reshape(K // 128, 128, C).transpose(1, 0, 2))


def _host_weights(inp):
    wfl = np.zeros((2, 128, TOT), np.float32)
    wsm = np.zeros((2, 128, NSM), np.float32)
    vecs = np.zeros((128, 2 * NV), np.float32)
    kr = OFF_KR + np.arange(32)
    kr_sw = OFF_KR + np.concatenate([np.arange(16, 32), np.arange(0, 16)])
    for l in range(2):
        W_in = np.asarray(inp["w_in"][l])

        def put(name, arr):
            o, s = POFF[name]
            a = arr.reshape(128, -1)
            assert a.shape[1] == s, (name, a.shape, s)
            wfl[l, :, o:o + s] = a
        idx0 = np.concatenate([np.arange(0, 384), kr_sw, kr, kr, kr_sw])
        put("in0", _lay_kc(W_in[:, idx0]))
        idx1 = np.concatenate([OFF_KV + np.arange(256), OFF_P + np.arange(256)])
        put("in1", _lay_kc(W_in[:, idx1]))
        put("in2", _lay_kc(W_in[:, OFF_P + 256:OFF_P + 512]))
        wq = np.asarray(inp["w_uq"][l])
        dsel = np.concatenate([np.arange(96), 64 + np.arange(16, 32), 64 + np.arange(0, 16)])
        wq2 = wq[:, :, dsel].reshape(384, 8 * 128)
        put("uq", _lay_kc(wq2))
        wao = np.asarray(inp["w_attn_out"][l]).reshape(4, 2, 64, 1024)
        wpo = np.asarray(inp["w_pool_out"][l])
        for c in range(8):
            cs = slice(c * 128, (c + 1) * 128)
            blk = np.zeros((128, 3072), np.float32)
            blk[:, 0:512] = wao[:, :, :, cs].transpose(1, 2, 0, 3).reshape(128, 512)
            blk[:, 512:1024] = _lay_kc(wpo[:, cs]).reshape(128, 512)
            blk[:, 1024:2048] = _lay_kc(W_in[:, OFF_GA + c * 128:OFF_GA + (c + 1) * 128]).reshape(128, 1024)
            blk[:, 2048:3072] = _lay_kc(W_in[:, OFF_GB + c * 128:OFF_GB + (c + 1) * 128]).reshape(128, 1024)
            put("mg%d" % c, blk)
        wo = np.asarray(inp["w_o"][l])
        for j in range(2):
            put("o%d" % j, _lay_kc(wo[:, j * 512:(j + 1) * 512]))
        wup = np.asarray(inp["w_up"][l])
        for j in range(8):
            put("up%d" % j, _lay_kc(wup[:, j * 512:(j + 1) * 512]))
        wdn = np.asarray(inp["w_down"][l])
        for j in range(8):
            put("dn%d" % j, _lay_kc(wdn[:, j * 128:(j + 1) * 128]))
        wkv = np.asarray(inp["w_ukv"][l])
        wsm[l, :, 0:1024] = _lay_kc(wkv[:, :, 0:64].reshape(256, 512)).reshape(128, 1024)
        wsm[l, :, 1024:2048] = _lay_kc(wkv[:, :, 64:128].reshape(256, 512)).reshape(128, 1024)
        wsm[l, :, 2048:2560] = np.asarray(inp["w_pool"][l]).transpose(1, 0, 2).reshape(128, 512)
        b = l * NV
        vecs[:, b + 0:b + 8] = np.asarray(inp["g_mix"][l]).reshape(8, 128).T
        vecs[:, b + 8:b + 16] = np.asarray(inp["g_mlp"][l]).reshape(8, 128).T
        vecs[:, b + 16:b + 19] = np.asarray(inp["g_qa"][l]).reshape(3, 128).T
        vecs[:, b + 19:b + 21] = np.asarray(inp["g_kva"][l]).reshape(2, 128).T
        vecs[:, b + 21:b + 25] = np.asarray(inp["pool_scale"][l]).reshape(4, 128).T
        vecs[0:96, b + 25] = np.asarray(inp["g_q"][l])
        vecs[0:96, b + 26] = np.asarray(inp["g_k"][l])
    return wfl, wsm, vecs


def _const_tables():
    cst = np.zeros((128, NCST), np.float32)
    cst[64:128, 0] = NEG
    for s in range(NSEQ_S):
        cst[:, 1 + s] = NEG
        cst[s * 16:(s + 1) * 16, 1 + s] = 0.0
    for g, w in enumerate(WINS):
        j = np.arange(16)
        cst[:, 5 + g * 16:5 + (g + 1) * 16] = (1.0 / np.minimum(w, j + 1).astype(np.float32))[None, :]
    half = RD // 2
    inv = np.power(np.float32(10000.0), -np.arange(half, dtype=np.float32) / np.float32(half)).astype(np.float32)
    rope = np.zeros((NT_FULL + 1, 128, 2, T), np.float32)
    for ti in range(NT_FULL + 1):
        if ti < NT_FULL:
            pos = (ti * T + np.arange(T)).astype(np.float32)
        else:
            pos = np.zeros(T, np.float32)
            pos[:TSAMP] = (PAST + (np.arange(TSAMP) % TS)).astype(np.float32)
        ang = (pos[None, :] * inv[:, None]).astype(np.float32)
        c = np.cos(ang).astype(np.float32)
        s = np.sin(ang).astype(np.float32)
        cos32 = np.concatenate([c, c], 0)
        sin32 = np.concatenate([-s, s], 0)
        for q in range(4):
            rope[ti, q * 32:(q + 1) * 32, 0] = cos32
            rope[ti, q * 32:(q + 1) * 32, 1] = sin32
    return cst, rope


_CACHE = {}


def _get_program(NT=NT_FULL, do_sample=True):
    key = (NT, do_sample)
    if key not in _CACHE:
        _CACHE[key] = build_program(NT, do_sample)
    return _CACHE[key]


def kernel(**inp):
    return _run(inp, NT_FULL, True)


def _run(inp, NT, do_sample, trace=False):
    inp = {k: np.asarray(v) for k, v in inp.items()}
    nc, _ = build_program(NT, do_sample)
    wfl, wsm, vecs = _host_weights(inp)
    cst, rope = _const_tables()
    xp = inp["x_prompt"]
    xs = inp["x_sample"]
    in_maps = []
    for c in range(8):
        xpT = np.ascontiguousarray(xp[c].reshape(NT_FULL, T, 8, 128).transpose(0, 3, 2, 1))
        xsT = np.ascontiguousarray(xs[4 * c:4 * c + 4].reshape(TSAMP, 8, 128).transpose(2, 1, 0))
        ck = inp["cache_ckv"][:, 4 * c:4 * c + 4]
        cckvT = np.ascontiguousarray(ck.reshape(2, 4, PAST, 2, 128).transpose(0, 1, 4, 3, 2))
        ckrT = np.ascontiguousarray(inp["cache_krope"][:, 4 * c:4 * c + 4].transpose(0, 1, 3, 2))
        sp = inp["state_pool"][:, 4 * c:4 * c + 4]
        spT = np.ascontiguousarray(sp.reshape(2, 4, 15, 4, 128).transpose(0, 4, 3, 1, 2))
        in_maps.append({"xpT": xpT, "xsT": xsT, "cckvT": cckvT, "ckrT": ckrT, "spT": spT, "wfl": wfl, "wsm": wsm,
                        "vecs": vecs, "cst": cst, "ropeT": rope})
    res = run_bass_kernel_spmd(nc, in_maps, core_ids=list(range(8)), **({"trace": True} if trace else {}))
    R = res.results
    y_p = np.stack([R[c]["ypT"].transpose(0, 3, 2, 1).reshape(SEQ, D) for c in range(8)])
    y_s = np.concatenate([R[c]["ysT"].transpose(2, 1, 0).reshape(4, TS, D) for c in range(8)])
    ckv_p = np.stack([R[c]["ckvpT"].transpose(0, 1, 4, 3, 2).reshape(2, SEQ, KVL) for c in range(8)], 1)
    kr_p = np.stack([R[c]["krpT"].transpose(0, 1, 3, 2).reshape(2, SEQ, RD) for c in range(8)], 1)
    pool_p = np.stack([R[c]["poolpT"].transpose(0, 3, 2, 1).reshape(2, 15, PW) for c in range(8)], 1)
    ckv_s = np.concatenate([R[c]["ckvsT"].transpose(0, 3, 2, 1).reshape(2, 4, TS, KVL) for c in range(8)], 1)
    kr_s = np.concatenate([R[c]["krsT"].transpose(0, 2, 1).reshape(2, 4, TS, RD) for c in range(8)], 1)
    pool_s = np.concatenate([R[c]["poolsT"].transpose(0, 3, 4, 2, 1).reshape(2, 4, 15, PW) for c in range(8)], 1)
    outs = (y_p, y_s, ckv_p, kr_p, pool_p, ckv_s, kr_s, pool_s)
    outs = tuple(np.ascontiguousarray(o, dtype=np.float32) for o in outs)
    if trace:
        return outs, res
    return outs
```

```python
import numpy as np
import concourse.bass as bass
import concourse.mybir as mybir
from concourse.bass_utils import run_bass_kernel_spmd

F32 = mybir.dt.float32
BF16 = mybir.dt.bfloat16
AF = mybir.ActivationFunctionType
ALU = mybir.AluOpType

D = 1024
SEQ = 4096
T = 512
NT_FULL = SEQ // T
NH = 8
QL, KVL, RD, PW = 384, 256, 32, 512
OFF_Q, OFF_KV, OFF_KR, OFF_P, OFF_GA, OFF_GB = 0, 384, 640, 672, 1184, 2208
DFF = 4096
EPS = 1e-6
SM_SCALE = 96 ** -0.5
WINS = (2, 4, 8, 16)
NSEQ_S, TS = 4, 16
TSAMP = NSEQ_S * TS
PAST = 4096
NEG = -30000.0

PIECES = [("in0", 4096), ("in1", 4096), ("in2", 2048), ("uq", 3072)]
PIECES += [("mg%d" % c, 3072) for c in range(8)]
PIECES += [("o0", 4096), ("o1", 4096)]
PIECES += [("up%d" % j, 4096) for j in range(8)]
PIECES += [("dn%d" % j, 4096) for j in range(8)]
POFF = {}
_o = 0
for _n, _s in PIECES:
    POFF[_n] = (_o, _s)
    _o += _s
TOT = _o
NSM = 2560
SLOT = 4096
NV = 27
NCST = 5 + 64


DEBUG_STOP = None


class _Stop(Exception):
    pass


def _stage(name):
    if DEBUG_STOP == name:
        raise _Stop()


class Buf:
    __slots__ = ("name", "w", "r", "excl")

    def __init__(self, name, excl=False):
        self.name = name
        self.w = None
        self.r = []
        self.excl = excl


class Prog:
    ENGS = ("pe", "act", "dve", "pool", "sp")

    def __init__(self):
        self.lists = {e: [] for e in self.ENGS}
        self.nops = {e: 0 for e in self.ENGS}
        self.seen = {e: {} for e in self.ENGS}
        self.dcount = {}
        self.milestones = {e: set() for e in self.ENGS}

    def _collect(self, eng, reads, writes):
        toks = []
        for b in reads:
            if b.w is not None:
                toks.append(b.w)
            if b.excl:
                for t in b.r:
                    if t[0] != eng:
                        toks.append(t)
        same_ok = False
        for b in writes:
            if b.w is not None and (b.w[0] != eng or not same_ok):
                toks.append(b.w)
            for t in b.r:
                if t[0] != eng or not same_ok:
                    toks.append(t)
        waits = {}
        seen = self.seen[eng]
        for k, v in toks:
            if k == "pe" and eng == "pe":
                continue
            if seen.get(k, 0) >= v:
                continue
            if waits.get(k, 0) < v:
                waits[k] = v
        for k, v in waits.items():
            seen[k] = v
            if k in self.milestones:
                self.milestones[k].add(v)
        return list(waits.items())

    def op(self, eng, fn, reads=(), writes=()):
        waits = self._collect(eng, reads, writes)
        self.nops[eng] += 1
        tok = (eng, self.nops[eng])
        self.lists[eng].append((waits, fn, None))
        for b in reads:
            b.r.append(tok)
        for b in writes:
            b.w = tok
            b.r = []
        return tok

    def dma(self, eng, fn, semkey, reads=(), writes=(), batch=False):
        waits = self._collect(eng, reads, writes)
        cnt = self.dcount.get(semkey, 0)
        if not batch and cnt > 0 and self.seen[eng].get(semkey, 0) < cnt:
            waits.append((semkey, cnt))
            self.seen[eng][semkey] = cnt
        cnt += 16
        self.dcount[semkey] = cnt
        tok = (semkey, cnt)
        self.nops[eng] += 1
        self.lists[eng].append((waits, fn, semkey))
        for b in reads:
            b.r.append(tok)
        for b in writes:
            b.w = tok
            b.r = []
        return tok

    @staticmethod
    def handoff(frm, to):
        toks = []
        for b in frm:
            if b.w is not None:
                toks.append(b.w)
            toks += b.r
        for b in to:
            b.r = list(b.r) + toks

    def emit(self, eng, e, sems):
        rank = {}
        for k in self.ENGS:
            ms = sorted(self.milestones[k])
            rank[k] = {v: i + 1 for i, v in enumerate(ms)}
        idx = 0
        for waits, fn, semkey in self.lists[eng]:
            idx += 1
            for k, v in waits:
                if k in rank:
                    e.wait_ge(sems[k], rank[k][v])
                else:
                    e.wait_ge(sems[k], v)
            ins = fn(e)
            if semkey is not None:
                ins.then_inc(sems[semkey], 16)
            elif idx in self.milestones[eng]:
                ins.then_inc(sems[eng], 1)


def build_program(NT=NT_FULL, do_sample=True):
    nc = bass.Bass("TRN2", target_bir_lowering=False)
    P = Prog()

    def din(name, shape, dt=F32):
        return nc.dram_tensor(name, list(shape), dt, kind="ExternalInput").ap()

    def dout(name, shape, dt=F32):
        return nc.dram_tensor(name, list(shape), dt, kind="ExternalOutput").ap()

    xpT = din("xpT", [NT_FULL, 128, 8, T])
    xsT = din("xsT", [128, 8, TSAMP])
    cckvT = din("cckvT", [2, NSEQ_S, 128, 2, PAST])
    ckrT = din("ckrT", [2, NSEQ_S, RD, PAST])
    spT = din("spT", [2, 128, 4, NSEQ_S, 15])
    wfl = din("wfl", [2, 128, TOT])
    wsm = din("wsm", [2, 128, NSM])
    vecs = din("vecs", [128, 2 * NV])
    cst = din("cst", [128, NCST])
    ropeT = din("ropeT", [NT_FULL + 1, 128, 2, T])

    ypT = dout("ypT", [NT_FULL, 128, 8, T])
    ysT = dout("ysT", [128, 8, TSAMP])
    ckvpT = dout("ckvpT", [2, NT_FULL, 128, 2, T])
    krpT = dout("krpT", [2, NT_FULL, RD, T])
    poolpT = dout("poolpT", [2, 128, 4, 15])
    ckvsT = dout("ckvsT", [2, 128, 2, TSAMP])
    krsT = dout("krsT", [2, RD, TSAMP])
    poolsT = dout("poolsT", [2, 128, 4, NSEQ_S, 15])

    wsc = nc.dram_tensor("wsc", [2, 128, TOT], BF16).ap()
    wsmb = nc.dram_tensor("wsmb", [2, 128, NSM], BF16).ap()
    kscr = nc.dram_tensor("kscr", [2, NT_FULL, 4, 96, 2, T], BF16).ap()
    vscr = nc.dram_tensor("vscr", [2, NT_FULL, 4, 128, 4, 2, 65], BF16).ap()
    rkscr = nc.dram_tensor("rkscr", [2, NT_FULL, 128, 4, 8], F32).ap()

    def sb(name, shape, dt):
        return nc.alloc_sbuf_tensor(name, list(shape), dt)

    ones_bf = sb("ones_bf", [128, 128], BF16)
    ones_f = sb("ones_f", [128, 64], F32)
    epsc = sb("epsc", [128, 1], F32)
    m01 = sb("m01", [128, 1], F32)
    sel = sb("sel", [128, 4, 8], BF16)
    vec_sb = sb("vec_sb", [128, 2 * NV], F32)
    cst_sb = sb("cst_sb", [128, NCST], F32)
    gqk = sb("gqk", [128, 2], F32)
    wsm_sb = sb("wsm_sb", [128, 2, NSM], BF16)
    hist = sb("hist", [128, 2, 4, 16], F32)
    xT = sb("xT", [128, 8, T], F32)
    rope = sb("rope", [128, 2, T], F32)
    actbf = sb("actbf", [128, 8, T], BF16)
    NF = 6
    ftmp = [sb("ftmp%d" % i, [128, T], F32) for i in range(NF)]
    sq3 = sb("sq3", [128, 3, T], BF16)
    cqn = sb("cqn", [128, 3, T], BF16)
    ckvb = sb("ckvb", [128, 2, T], BF16)
    ckvf = sb("ckvf", [128, 2, T], F32)
    krr = sb("krr", [128, T], F32)
    ksqr = sb("ksqr", [128, T], BF16)
    pext = sb("pext", [128, 4, 16 + T], F32)
    ptmp = [sb("ptmp%d" % i, [128, 16 + T], F32) for i in range(2)]
    pooled = sb("pooled", [128, 4, T], BF16)
    u_sb = sb("u_sb", [128, 4, T], BF16)
    qf = sb("qf", [128, T], F32)
    qsq = sb("qsq", [128, T], BF16)
    Kown = sb("Kown", [128, 8, T], BF16)
    ksq = [sb("ksq%d" % i, [128, T], BF16) for i in range(4)]
    Vown = sb("Vown", [128, 4, 8, 65], BF16)
    rk_own = sb("rk_own", [128, 4, 8], F32)
    rk_tmp = sb("rk_tmp", [128, 4, 8], F32)
    NKV = 2
    KVE = 4096 + 2080
    big = sb("big", [128, NKV * KVE + 4096], BF16)
    rk_slot = [sb("rk_slot%d" % i, [128, 4, 8], F32) for i in range(NKV)]
    rk_sub = [sb("rk_sub%d" % i, [128, 4, 8], F32) for i in range(4)]
    cstage = [sb("cstage%d" % i, [128, 2, T], F32) for i in range(2)]
    krstage = [sb("krstage%d" % i, [128, T], F32) for i in range(2)]
    NPT = 6
    PT = [sb("PT%d" % i, [128, T], BF16) for i in range(NPT)]
    attn = sb("attn", [128, 4, T], BF16)
    m_sb = sb("m_sb", [128, 8, T], BF16)
    NW = 4
    wslot = [sb("wslot%d" % i, [128, SLOT], BF16) for i in range(NW)]

    def kslot(i):
        return big[:, i * KVE:i * KVE + 4096].rearrange("p (h t) -> p h t", h=8)

    def vslot(i):
        return big[:, i * KVE + 4096:(i + 1) * KVE].rearrange("p (b h d) -> p b h d", b=4, h=8, d=65)

    Qv = big[:, NKV * KVE:NKV * KVE + 4096].rearrange("p (h t) -> p h t", h=8)
    aT = big[:, 0:32 * T].rearrange("p (k t) -> p k t", k=32)

    banks = [nc.alloc_psum_tensor("bank%d" % i, [128, 512], F32) for i in range(8)]

    B = {}

    def buf(name):
        if name not in B:
            B[name] = Buf(name)
        return B[name]

    bank_b = [buf("bank%d" % i) for i in range(8)]
    for b_ in bank_b:
        b_.excl = True
    ftmp_b = [buf("ftmp%d" % i) for i in range(NF)]
    pt_b = [buf("PT%d" % i) for i in range(NPT)]
    wslot_b = [buf("wslot%d" % i) for i in range(NW)]
    kv_b = [buf("kvslot%d" % i) for i in range(NKV)]
    kvs_b = [buf("kvsub%d" % i) for i in range(4)]
    a_b = [buf("aT%d" % i) for i in range(32)]
    x_b = [buf("x%d" % i) for i in range(8)]
    act_b = [buf("act%d" % i) for i in range(8)]
    m_b = [buf("m%d" % i) for i in range(8)]

    state = {"bank": 0, "ft": 0, "pt": 0, "attn_mode": False, "pinned": set(), "nbank": 8}

    def alloc_bank():
        n = state["nbank"]
        while True:
            i = state["bank"] % n
            state["bank"] += 1
            if i not in state["pinned"]:
                return i

    ft_hold = set()

    def alloc_ft():
        while True:
            i = state["ft"] % NF
            state["ft"] += 1
            if i not in ft_hold:
                return i

    def alloc_pt():
        i = state["pt"] % NPT
        state["pt"] += 1
        return i

    def mm(out, lhsT, rhs, start, stop, reads, writes):
        P.op("pe", lambda e, o=out, l=lhsT, r=rhs, s=start, st=stop:
             e.matmul(o, lhsT=l, rhs=r, start=s, stop=st, skip_group_check=True),
             reads=reads, writes=writes)

    def act(out, in_, func, reads, writes, scale=None, bias=None):
        def fn(e, o=out, i=in_, f=func, s=scale, b=bias):
            kw = {}
            if s is not None:
                kw["scale"] = s
            if b is not None:
                kw["bias"] = b
            return e.activation(out=o, in_=i, func=f, **kw)
        P.op("act", fn, reads=reads, writes=writes)

    def tt(eng, out, in0, in1, op, reads, writes):
        P.op(eng, lambda e, o=out, a=in0, b=in1, p=op: e.tensor_tensor(out=o, in0=a, in1=b, op=p),
             reads=reads, writes=writes)

    def stt(eng, out, in0, scalar, in1, op0, op1, reads, writes):
        P.op(eng, lambda e, o=out, a=in0, s=scalar, b=in1, p0=op0, p1=op1:
             e.scalar_tensor_tensor(out=o, in0=a, scalar=s, in1=b, op0=p0, op1=p1),
             reads=reads, writes=writes)

    def ts(eng, out, in0, s1, s2, op0, op1, reads, writes):
        def fn(e, o=out, a=in0, x=s1, y=s2, p0=op0, p1=op1):
            if p1 is None:
                return e.tensor_scalar(out=o, in0=a, scalar1=x, scalar2=None, op0=p0)
            return e.tensor_scalar(out=o, in0=a, scalar1=x, scalar2=y, op0=p0, op1=p1)
        P.op(eng, fn, reads=reads, writes=writes)

    def cp(eng, out, in_, reads, writes):
        P.op(eng, lambda e, o=out, i=in_: e.tensor_copy(out=o, in_=i), reads=reads, writes=writes)

    def copy_any(eng, out, in_, reads, writes):
        if eng == "act":
            act(out, in_, AF.Copy, reads, writes)
        else:
            cp(eng, out, in_, reads, writes)

    def recip(out, in_, reads, writes):
        P.op("dve", lambda e, o=out, i=in_: e.reciprocal(out=o, in_=i), reads=reads, writes=writes)

    def memset(eng, ap, val, writes):
        P.op(eng, lambda e, a=ap, v=val: e.memset(a, v), writes=writes)

    def dma(eng, out, in_, semkey, reads, writes, batch=False):
        return P.dma(eng, lambda e, o=out, i=in_: e.dma_start(out=o, in_=i), semkey,
                     reads=reads, writes=writes, batch=batch)

    cb = buf("consts")
    hist_b = [buf("hist0"), buf("hist1")]
    memset("pool", pext[:], 0.0, [buf("pext")])
    memset("pool", hist[:, 0], 0.0, [hist_b[0]])
    memset("pool", hist[:, 1], 0.0, [hist_b[1]])
    vown_b = buf("Vown")
    memset("pool", Vown[:, :, :, 64:65], 1.0, [vown_b])
    dma("sp", vec_sb[:], vecs, "ld_misc", [], [cb], batch=True)
    dma("sp", cst_sb[:], cst, "ld_misc", [], [cb], batch=True)
    conv_b = {}
    groups = [("A", ["in0", "in1", "in2", "uq"])]
    _bn = ["mg%d" % c for c in range(8)] + ["o0", "o1"]
    groups += [("B%d" % i, _bn[2 * i:2 * i + 2]) for i in range(5)]
    groups += [("C%d" % i, ["up%d" % (2 * i), "up%d" % (2 * i + 1)]) for i in range(4)]
    groups += [("D%d" % i, ["dn%d" % (2 * i), "dn%d" % (2 * i + 1)]) for i in range(4)]
    conv_sched = [1, 5, 4, 4, 1, 5, 4, 4]
    smb = buf("wsmb")
    pending_conv = []

    def issue_conv(l, gname, names):
        key = "cv_%s%d" % (gname, l)
        a = POFF[names[0]][0]
        b_ = POFF[names[-1]][0] + POFF[names[-1]][1]
        gb = conv_b[(gname, l)]
        step = 8192
        o = a
        while o < b_:
            e_ = min(b_, o + step)
            dma("pool", wsc[l, :, o:e_], wfl[l, :, o:e_], key, [], [gb], batch=True)
            o = e_
        gb.w = (key, P.dcount[key])

    dma("pool", wsmb, wsm, "cv_sm", [], [smb], batch=True)
    for l in range(2):
        for gname, names in groups:
            conv_b[(gname, l)] = buf("cv_%s%d" % (gname, l))
            pending_conv.append((l, gname, names))

    def conv_step():
        n = conv_sched.pop(0) if conv_sched else len(pending_conv)
        for _ in range(n):
            if pending_conv:
                issue_conv(*pending_conv.pop(0))

    conv_step()
    piece_group = {}
    for gname, names in groups:
        for n in names:
            piece_group[n] = gname
    smb.w = ("cv_sm", P.dcount["cv_sm"])
    wsm_b = buf("wsm_sb")
    dma("sp", wsm_sb[:], wsmb.rearrange("l p n -> p l n"), "ld_misc", [smb], [wsm_b], batch=True)
    for b_ in (cb, wsm_b):
        b_.w = ("ld_misc", P.dcount["ld_misc"])
    memset("dve", ones_bf[:], 1.0, [cb])
    memset("dve", ones_f[:], 1.0, [cb])
    memset("dve", epsc[:], EPS, [cb])
    memset("dve", m01[0:64, :], 1.0, [cb])
    memset("dve", m01[64:128, :], 0.0, [cb])
    memset("dve", sel[:], 0.0, [cb])
    for p_ in range(4):
        memset("dve", sel[0:64, p_, 2 * p_:2 * p_ + 1], 1.0, [cb])
        memset("dve", sel[64:128, p_, 2 * p_ + 1:2 * p_ + 2], 1.0, [cb])
    for l in range(2):
        tt("dve", gqk[0:96, l:l + 1], vec_sb[0:96, l * NV + 25:l * NV + 26], vec_sb[0:96, l * NV + 26:l * NV + 27],
           ALU.mult, [cb], [buf("gqk")])
    ts("dve", gqk[0:96, :], gqk[0:96, :], SM_SCALE, None, ALU.mult, None, [buf("gqk")], [buf("gqk")])

    def vcol(l, off, c=0):
        return vec_sb[:, l * NV + off + c:l * NV + off + c + 1]
    V_GMIX, V_GMLP, V_GQA, V_GKVA, V_PS = 0, 8, 16, 19, 21
    maskd = cst_sb[:, 0:1]

    def ukvK(l, kc, h):
        return wsm_sb[:, l, kc * 512 + h * 64:kc * 512 + (h + 1) * 64]

    def ukvV(l, kc):
        return wsm_sb[:, l, 1024 + kc * 512:1024 + (kc + 1) * 512]

    def wpool(l, g):
        return wsm_sb[:, l, 2048 + g * 128:2048 + (g + 1) * 128]

    tiles = [("p", t) for t in range(NT)] + ([("s", 0)] if do_sample else [])
    wstream = []
    for kind, t in tiles:
        for l in range(2):
            for n, s in PIECES:
                wstream.append((l, n))
    wst = {"cur": 0, "rec": 0}

    def next_piece(expect):
        idx = wst["cur"]
        while wst["rec"] < min(len(wstream), idx + NW):
            j = wst["rec"]
            l, n = wstream[j]
            o, s = POFF[n]
            si = j % NW
            dma("sp", wslot[si][:, 0:s], wsc[l, :, o:o + s], "wslot%d" % si,
                [conv_b[(piece_group[n], l)]], [wslot_b[si]])
            wst["rec"] += 1
        assert wstream[idx][1] == expect, (wstream[idx], expect)
        wst["cur"] += 1
        return wslot[idx % NW], wslot_b[idx % NW]

    def rmsnorm_x(l, gcol, Tn):
        for c in range(8):
            act(actbf[:, c, :Tn], xT[:, c, :Tn], AF.Square, [x_b[c]], [act_b[c]])
        bi = alloc_bank()
        for c in range(8):
            mm(banks[bi][:, :Tn], ones_bf[:, :], actbf[:, c, :Tn], c == 0, c == 7, [act_b[c], cb], [bank_b[bi]])
        f1 = alloc_ft()
        act(ftmp[f1][:, :Tn], banks[bi][:, :Tn], AF.Ln, [bank_b[bi], cb], [ftmp_b[f1]], scale=1.0 / D, bias=epsc[:, 0:1])
        f2 = alloc_ft()
        act(ftmp[f2][:, :Tn], ftmp[f1][:, :Tn], AF.Exp, [ftmp_b[f1]], [ftmp_b[f2]], scale=-0.5)
        for c in range(8):
            stt("dve", actbf[:, c, :Tn], xT[:, c, :Tn], vcol(l, gcol, c), ftmp[f2][:, :Tn], ALU.mult, ALU.mult,
                [x_b[c], ftmp_b[f2], cb], [act_b[c]])

    def kv_generate(l, src_bf, src_b, kr_f, kr_b, nk, Kd, Vd, rkd, kd_b, extra_w=(), set_ones=False, bcast_rope=True):
        nkb = (nk + 127) // 128
        kw = nk if nk < 128 else 128
        if bcast_rope:
            act(Kd[64:96, :, :nk], kr_f[64:96, :nk].unsqueeze(1).broadcast_to([32, 8, nk]), AF.Copy, [kr_b], [kd_b])
        else:
            cp("dve", Kd[64:96, 0, :nk], kr_f[64:96, :nk], [kr_b], [kd_b])
        ksr_b = buf("ksqr")
        tt("pool", ksqr[64:96, :nk], kr_f[64:96, :nk], kr_f[64:96, :nk], ALU.mult, [kr_b], [ksr_b])
        _stage("kvA")
        rb = alloc_bank()
        state["pinned"].add(rb)
        psr = banks[rb][:, 0:32].rearrange("p (b h) -> p b h", b=4)
        memset("dve", banks[rb][:, 0:32], 0.0, [bank_b[rb]])
        if set_ones:
            memset("pool", Vd[:, :, :, 64:65], 1.0, [kd_b])
        kbanks = []
        for hp in range(4):
            bi = alloc_bank()
            kbanks.append(bi)
            for kc in range(2):
                mm(banks[bi][:, :nk], wsm_sb[:, l, kc * 512 + hp * 128:kc * 512 + (hp + 1) * 128], src_bf[:, kc, :nk],
                   kc == 0, kc == 1, [src_b, wsm_b], [bank_b[bi]])
        for hp in range(4):
            bi = kbanks[hp]
            cp("dve", Kd[0:64, 2 * hp + 1, :nk], banks[bi][64:128, :nk], [bank_b[bi]], [kd_b])
            act(Kd[0:64, 2 * hp, :nk], banks[bi][0:64, :nk], AF.Copy, [bank_b[bi]], [kd_b])
            act(ksq[hp][:, :nk], banks[bi][:, :nk], AF.Square, [bank_b[bi]], [buf("ksq%d" % hp)])
        for kb in range(nkb):
            bi = alloc_bank()
            for kc in range(2):
                mm(banks[bi][0:kw, :], src_bf[:, kc, kb * 128:kb * 128 + kw], ukvV(l, kc), kc == 0, kc == 1,
                   [src_b, wsm_b], [bank_b[bi]])
            copy_any("act" if kb % 2 else "dve", Vd[0:kw, kb, :, 0:64], banks[bi][0:kw, :].rearrange("p (h d) -> p h d", h=8),
                     [bank_b[bi]], [kd_b] + list(extra_w))
        for hp in range(4):
            for kb in range(nkb):
                mm(psr[0:kw, kb, :], ksq[hp][:, kb * 128:kb * 128 + kw], sel[:, hp, :], False, False,
                   [buf("ksq%d" % hp), cb], [bank_b[rb]])
                if hp == 0:
                    mm(psr[0:kw, kb, :], ksqr[64:96, kb * 128:kb * 128 + kw], ones_bf[64:96, 0:8], False, False,
                       [ksr_b, cb], [bank_b[rb]])
        rtb = buf("rk_tmp")
        act(rk_tmp[0:kw, 0:nkb, :], psr[0:kw, 0:nkb, :], AF.Ln, [bank_b[rb], cb], [rtb], scale=1.0 / 96, bias=epsc[0:kw, 0:1])
        act(rkd[0:kw, 0:nkb, :], rk_tmp[0:kw, 0:nkb, :], AF.Exp, [rtb], [kd_b], scale=-0.5)
        state["pinned"].discard(rb)
        _stage("kvE")

    def tile_layer(kind, t, l, first_kv_handoff):
        samp = kind == "s"
        Tn = TSAMP if samp else T
        nseg, seg = (NSEQ_S, TS) if samp else (1, T)
        ri = NT_FULL if samp else t

        if first_kv_handoff:
            Prog.handoff(a_b, kv_b + kvs_b + [buf("Q")])
            if samp and l == 0:
                Prog.handoff(kvs_b, kv_b)

        if l == 0:
            for c in range(8):
                pass
            src = xsT if samp else xpT[t]
            for c in range(8):
                dma("sp", xT[:, c, :Tn], src[:, c, :Tn], "ld_x", [], [x_b[c]])
            dma("sp", rope[:, :, :Tn], ropeT[ri][:, :, :Tn], "ld_rope", [], [buf("rope")])
        rope_b = buf("rope")
        cosT = rope[:, 0, :Tn]
        sinT = rope[:, 1, :Tn]

        _stage("setup")
        rmsnorm_x(l, V_GMIX, Tn)

        _stage("norm1")
        ws, wb = next_piece("in0")
        w3 = ws[:, 0:4096].rearrange("p (k c) -> p k c", k=8)
        cqbanks = []
        for c in range(3):
            bi = alloc_bank()
            cqbanks.append(bi)
            for kc in range(8):
                mm(banks[bi][:, :Tn], w3[:, kc, c * 128:(c + 1) * 128], actbf[:, kc, :Tn], kc == 0, kc == 7,
                   [wb, act_b[kc]], [bank_b[bi]])
            act(sq3[:, c, :Tn], banks[bi][:, :Tn], AF.Square, [bank_b[bi]], [buf("sq3")])
        nb = alloc_bank()
        for c in range(3):
            mm(banks[nb][:, :Tn], ones_bf[:, :], sq3[:, c, :Tn], c == 0, c == 2, [buf("sq3"), cb], [bank_b[nb]])
        f1 = alloc_ft()
        act(ftmp[f1][:, :Tn], banks[nb][:, :Tn], AF.Ln, [bank_b[nb], cb], [ftmp_b[f1]], scale=1.0 / QL, bias=epsc[:, 0:1])
        f2 = alloc_ft()
        act(ftmp[f2][:, :Tn], ftmp[f1][:, :Tn], AF.Exp, [ftmp_b[f1]], [ftmp_b[f2]], scale=-0.5)
        for c in range(3):
            bi = cqbanks[c]
            stt("dve", cqn[:, c, :Tn], banks[bi][:, :Tn], vcol(l, V_GQA, c), ftmp[f2][:, :Tn], ALU.mult, ALU.mult,
                [bank_b[bi], ftmp_b[f2], cb], [buf("cqn")])
        bi = alloc_bank()
        for kc in range(8):
            mm(banks[bi][:, :Tn], w3[:, kc, 384:512], actbf[:, kc, :Tn], kc == 0, kc == 7, [wb, act_b[kc]], [bank_b[bi]])
        f1 = alloc_ft()
        f2 = alloc_ft()
        tt("dve", ftmp[f1][64:96, :Tn], banks[bi][0:32, :Tn], sinT[0:32], ALU.mult, [bank_b[bi], rope_b], [ftmp_b[f1]])
        tt("dve", ftmp[f2][64:96, :Tn], banks[bi][64:96, :Tn], cosT[64:96], ALU.mult, [bank_b[bi], rope_b], [ftmp_b[f2]])
        krr_b = buf("krr")
        tt("dve", krr[64:96, :Tn], ftmp[f1][64:96, :Tn], ftmp[f2][64:96, :Tn], ALU.add, [ftmp_b[f1], ftmp_b[f2]], [krr_b])
        if samp:
            dma("pool", krsT[l], krr[64:96, :Tn], "st_kr", [krr_b], [])
        else:
            dma("pool", krpT[l, t], krr[64:96, :Tn], "st_kr", [krr_b], [])
        conv_step()
        ws, wb = next_piece("in1")
        w3 = ws[:, 0:4096].rearrange("p (k c) -> p k c", k=8)
        kvbanks = []
        for c in range(2):
            bi = alloc_bank()
            kvbanks.append(bi)
            for kc in range(8):
                mm(banks[bi][:, :Tn], w3[:, kc, c * 128:(c + 1) * 128], actbf[:, kc, :Tn], kc == 0, kc == 7,
                   [wb, act_b[kc]], [bank_b[bi]])
            act(sq3[:, c, :Tn], banks[bi][:, :Tn], AF.Square, [bank_b[bi]], [buf("sq3")])
        nb = alloc_bank()
        for c in range(2):
            mm(banks[nb][:, :Tn], ones_bf[:, :], sq3[:, c, :Tn], c == 0, c == 1, [buf("sq3"), cb], [bank_b[nb]])
        f1 = alloc_ft()
        act(ftmp[f1][:, :Tn], banks[nb][:, :Tn], AF.Ln, [bank_b[nb], cb], [ftmp_b[f1]], scale=1.0 / KVL, bias=epsc[:, 0:1])
        f2 = alloc_ft()
        act(ftmp[f2][:, :Tn], ftmp[f1][:, :Tn], AF.Exp, [ftmp_b[f1]], [ftmp_b[f2]], scale=-0.5)
        ckvf_b, ckvb_b = buf("ckvf"), buf("ckvb")
        for c in range(2):
            bi = kvbanks[c]
            stt("dve", ckvf[:, c, :Tn], banks[bi][:, :Tn], vcol(l, V_GKVA, c), ftmp[f2][:, :Tn], ALU.mult, ALU.mult,
                [bank_b[bi], ftmp_b[f2], cb], [ckvf_b])
        for c in range(2):
            bi = kvbanks[c]
            stt("dve", ckvb[:, c, :Tn], banks[bi][:, :Tn], vcol(l, V_GKVA, c), ftmp[f2][:, :Tn], ALU.mult, ALU.mult,
                [bank_b[bi], ftmp_b[f2], cb], [ckvb_b])
        if samp:
            dma("pool", ckvsT[l], ckvf[:, :, :Tn], "st_ckv", [ckvf_b], [])
        else:
            dma("pool", ckvpT[l, t], ckvf[:, :, :Tn], "st_ckv", [ckvf_b], [])
        pext_b = buf("pext")
        pv = pext[:, :, 0:nseg * (16 + seg)].rearrange("p g (s j) -> p g s j", s=nseg)
        if samp:
            for g in range(4):
                dma("sp", pv[:, g, :, 1:16], spT[l, :, g], "ld_hist", [], [pext_b], batch=(g > 0))
            pext_b.w = ("ld_hist", P.dcount["ld_hist"])
        else:
            cp("pool", pv[:, :, 0, 0:16], hist[:, l], [hist_b[l]], [pext_b])
        for g in range(4):
            if g == 2:
                ws, wb = next_piece("in2")
                w3 = ws[:, 0:2048].rearrange("p (k c) -> p k c", k=8)
            col = (256 + g * 128) if g < 2 else (g - 2) * 128
            bi = alloc_bank()
            for kc in range(8):
                mm(banks[bi][:, :Tn], w3[:, kc, col:col + 128], actbf[:, kc, :Tn], kc == 0, kc == 7,
                   [wb, act_b[kc]], [bank_b[bi]])
            act(pv[:, g, :, 16:16 + seg], banks[bi][:, :Tn].rearrange("p (s j) -> p s j", s=nseg), AF.Copy,
                [bank_b[bi]], [pext_b])
        if not samp:
            cp("pool", hist[:, l], pv[:, :, 0, seg:seg + 16], [pext_b], [hist_b[l]])
            if t == NT_FULL - 1:
                dma("pool", poolpT[l], pv[:, :, 0, 16 + seg - 15:16 + seg], "st_pool", [pext_b], [])
        else:
            for g in range(4):
                dma("pool", poolsT[l, :, g], pv[:, g, :, 17:32], "st_pool", [pext_b], [], batch=(g > 0))

        _stage("pool")
        conv_step()
        ws, wb = next_piece("uq")
        wq = ws[:, 0:3072].rearrange("p (k h c) -> p k h c", k=3, h=8)
        q_b = buf("Q")
        krr_q = krstage[1]
        qsq_t = [qsq, ksqr]
        qsq_bs = [buf("qsq"), buf("ksqr")]
        qf_t = [qf, krr_q]
        qf_bs = [buf("qf"), buf("cstage1")]
        for hp in range(4):
            hb = []
            for k2 in range(2):
                h = 2 * hp + k2
                bi = alloc_bank()
                hb.append(bi)
                for kc in range(3):
                    mm(banks[bi][:, :Tn], wq[:, kc, h, :], cqn[:, kc, :Tn], kc == 0, kc == 2, [wb, buf("cqn")], [bank_b[bi]])
                act(qsq_t[k2][0:96, :Tn], banks[bi][0:96, :Tn], AF.Square, [bank_b[bi]], [qsq_bs[k2]])
                f1 = alloc_ft()
                f2 = alloc_ft()
                tt("dve", ftmp[f1][64:96, :Tn], banks[bi][96:128, :Tn], sinT[96:128], ALU.mult, [bank_b[bi], rope_b], [ftmp_b[f1]])
                tt("dve", ftmp[f2][64:96, :Tn], banks[bi][64:96, :Tn], cosT[64:96], ALU.mult, [bank_b[bi], rope_b], [ftmp_b[f2]])
                tt("dve", qf_t[k2][64:96, :Tn], ftmp[f1][64:96, :Tn], ftmp[f2][64:96, :Tn], ALU.add,
                   [ftmp_b[f1], ftmp_b[f2]], [qf_bs[k2]])
            rs = []
            for k2 in range(2):
                nb = alloc_bank()
                mm(banks[nb][0:96, :Tn], ones_bf[0:96, 0:96], qsq_t[k2][0:96, :Tn], True, True, [qsq_bs[k2], cb], [bank_b[nb]])
                f3 = alloc_ft()
                act(ftmp[f3][0:96, :Tn], banks[nb][0:96, :Tn], AF.Ln, [bank_b[nb], cb], [ftmp_b[f3]], scale=1.0 / 96,
                    bias=epsc[0:96, 0:1])
                f4 = alloc_ft()
                ft_hold.add(f4)
                act(ftmp[f4][0:96, :Tn], ftmp[f3][0:96, :Tn], AF.Exp, [ftmp_b[f3]], [ftmp_b[f4]], scale=-0.5)
                rs.append(f4)
            for k2 in range(2):
                h = 2 * hp + k2
                bi, f4 = hb[k2], rs[k2]
                stt("dve", Qv[0:64, h, :Tn], banks[bi][0:64, :Tn], gqk[0:64, l:l + 1], ftmp[f4][0:64, :Tn], ALU.mult, ALU.mult,
                    [bank_b[bi], ftmp_b[f4], buf("gqk")], [q_b])
                stt("dve", Qv[64:96, h, :Tn], qf_t[k2][64:96, :Tn], gqk[64:96, l:l + 1], ftmp[f4][64:96, :Tn], ALU.mult, ALU.mult,
                    [qf_bs[k2], ftmp_b[f4], buf("gqk")], [q_b])
                ft_hold.discard(f4)

        _stage("q")
        conv_step()
        kown_b = buf("Kown")
        kv_generate(l, ckvb, ckvb_b, krr, krr_b, Tn, Kown, Vown, rk_own, kown_b, extra_w=[vown_b])
        vown_b.w = kown_b.w
        if (not samp) and t < NT - 1:
            sk = "st_kv%d" % l
            scr_b = [buf("scr_%d_%d_%d" % (l, t, hg)) for hg in range(4)]
            for hg in range(4):
                dma("pool", kscr[l, t, hg], Kown[0:96, hg * 2:(hg + 1) * 2, :], sk, [kown_b], [scr_b[hg]], batch=(hg > 0))
                dma("pool", vscr[l, t, hg], Vown[:, :, hg * 2:(hg + 1) * 2, :], sk, [kown_b, vown_b], [scr_b[hg]], batch=True)
            dma("pool", rkscr[l, t], rk_own[:], sk, [kown_b], scr_b, batch=True)
            for b_ in scr_b:
                b_.w = (sk, P.dcount[sk])

        _stage("win")
        W_ = 16 + seg
        pooled_b = buf("pooled")
        ptb = [buf("ptmp0"), buf("ptmp1")]
        for g, w in enumerate(WINS):
            cur, cur_b = pv[:, g], pext_b
            sh, k = 1, 0
            while sh < w:
                lo = 2 * sh - 1
                dst = ptmp[k][:, 0:nseg * W_].rearrange("p (s j) -> p s j", s=nseg)
                tt("pool", dst[:, :, lo:W_], cur[:, :, lo:W_], cur[:, :, lo - sh:W_ - sh], ALU.add, [cur_b], [ptb[k]])
                cur, cur_b = dst, ptb[k]
                k ^= 1
                sh *= 2
            stt("dve", pooled[:, g, :Tn].rearrange("p (s j) -> p s j", s=nseg), cur[:, :, 16:W_], 1.0 / w,
                pv[:, g, :, 16:W_], ALU.mult, ALU.subtract, [cur_b, pext_b], [pooled_b])
            if (not samp) and t == 0:
                f1 = alloc_ft()
                tt("pool", ftmp[f1][:, 0:16], cur[:, 0, 16:32], cst_sb[:, 5 + g * 16:5 + (g + 1) * 16], ALU.mult,
                   [cur_b, cb], [ftmp_b[f1]])
                tt("pool", pooled[:, g, 0:16], ftmp[f1][:, 0:16], pv[:, g, 0, 16:32], ALU.subtract,
                   [ftmp_b[f1], pext_b], [pooled_b])
        _stage("kv")
        attn_b = buf("attn")
        state["attn_mode"] = True
        state["bank"] = 0
        state["nbank"] = 6 if samp else 5

        def fin_a(ai, ncols):
            f2 = alloc_ft()
            ft_hold.add(f2)
            act(ftmp[f2][0:65, :ncols], banks[ai][0:65, :ncols], AF.Copy, [bank_b[ai]], [ftmp_b[f2]])
            return f2

        def fin_b(f2, ncols, out_ap, out_rearr, fbank=None):
            f1 = alloc_ft()
            recip(ftmp[f1][64:65, :ncols], ftmp[f2][64:65, :ncols], [ftmp_b[f2]], [ftmp_b[f1]])
            bi = alloc_bank() if fbank is None else fbank
            mm(banks[bi][0:64, :ncols], ones_f[64:65, 0:64], ftmp[f1][64:65, :ncols], True, True, [ftmp_b[f1], cb], [bank_b[bi]])
            i0 = ftmp[f2][0:64, :ncols]
            i1 = banks[bi][0:64, :ncols]
            if out_rearr:
                i0 = i0.rearrange("p (hp k q) -> p hp k q", hp=4, k=2)
                i1 = i1.rearrange("p (hp k q) -> p hp k q", hp=4, k=2)
                for k_ in range(2):
                    tt("dve", attn[k_ * 64:(k_ + 1) * 64, :, out_ap], i0[:, :, k_, :], i1[:, :, k_, :], ALU.mult,
                       [ftmp_b[f2], bank_b[bi]], [attn_b])
            else:
                h_ = out_ap
                tt("dve", attn[(h_ % 2) * 64:(h_ % 2) * 64 + 64, h_ // 2, :ncols], i0, i1, ALU.mult,
                   [ftmp_b[f2], bank_b[bi]], [attn_b])
            ft_hold.discard(f2)

        def finalize(ai, ncols, out_ap, out_rearr):
            fin_b(fin_a(ai, ncols), ncols, out_ap, out_rearr)

        if not samp:
            kvst = {"cur": 0, "rec": 0}
            kvlist = [(hg, j) for hg in range(4) for j in range(t)]

            def kv_acquire():
                idx = kvst["cur"]
                while kvst["rec"] < min(len(kvlist), idx + 3):
                    jj = kvst["rec"]
                    hg_, j_ = kvlist[jj]
                    ss = jj % 4
                    sb_ = buf("scr_%d_%d_%d" % (l, j_, hg_))
                    key = "kvsub%d" % ss
                    ho = (ss % 2) * 2
                    dma("sp", kslot(ss // 2)[0:96, ho:ho + 2, :], kscr[l, j_, hg_], key, [sb_], [kvs_b[ss]])
                    dma("sp", vslot(ss // 2)[:, :, ho:ho + 2, :], vscr[l, j_, hg_], key, [sb_], [kvs_b[ss]], batch=True)
                    dma("sp", rk_sub[ss][:], rkscr[l, j_], key, [sb_], [kvs_b[ss]], batch=True)
                    kvs_b[ss].w = (key, P.dcount[key])
                    kvst["rec"] += 1
                kvst["cur"] += 1
                return idx % 4

            blocks = [(hg, j, hh, kb) for hg in range(4) for j in range(t + 1) for hh in range(2) for kb in range(4)]
            srcs = {}

            def get_src(hg, j):
                if (hg, j) not in srcs:
                    if j != t:
                        ss = kv_acquire()
                        srcs[(hg, j)] = (kslot(ss // 2), vslot(ss // 2), rk_sub[ss], kvs_b[ss], (ss % 2) * 2, False)
                    else:
                        srcs[(hg, j)] = (Kown, Vown, rk_own, kown_b, hg * 2, True)
                return srcs[(hg, j)]

            def emit_S(b):
                hg, j, hh, kb = b
                Ks, Vs, rks, src_b, hoff, own = get_src(hg, j)
                h = hg * 2 + hh
                q0 = kb * 128 if own else 0
                N = T - q0
                bi = alloc_bank()
                mm(banks[bi][:, :N], Ks[0:96, hoff + hh, kb * 128:(kb + 1) * 128], Qv[0:96, h, q0:T], True, True,
                   [src_b, q_b], [bank_b[bi]])
                return bi, q0, N

            def emit_rest(b, info):
                hg, j, hh, kb = b
                bi, q0, N = info
                Ks, Vs, rks, src_b, hoff, own = get_src(hg, j)
                h = hg * 2 + hh
                ai = 6 + hh
                pi = alloc_pt()
                sc = rks[:, kb, h:h + 1]
                act(PT[pi][:, :N], banks[bi][:, :N], AF.Exp, [bank_b[bi], src_b], [pt_b[pi]], scale=sc)
                if own:
                    mm(banks[ai][0:65, q0:q0 + 64], Vs[0:64, kb, hoff + hh, :], PT[pi][0:64, 0:64], (j == 0 and kb == 0),
                       False, [pt_b[pi], src_b, vown_b], [bank_b[ai]])
                    mm(banks[ai][0:65, q0 + 64:T], Vs[:, kb, hoff + hh, :], PT[pi][:, 64:N], False,
                       kb == 3, [pt_b[pi], src_b, vown_b], [bank_b[ai]])
                else:
                    mm(banks[ai][0:65, q0:T], Vs[:, kb, hoff + hh, :], PT[pi][:, :N], (j == 0 and kb == 0),
                       False, [pt_b[pi], src_b], [bank_b[ai]])

            LOOK = 4
            infos = {}
            pend_fin = []
            for i in range(min(LOOK, len(blocks))):
                infos[i] = emit_S(blocks[i])
            for i, b in enumerate(blocks):
                while pend_fin and pend_fin[0][2] <= i:
                    f2_, h_, _ = pend_fin.pop(0)
                    fin_b(f2_, T, h_, False, fbank=5)
                emit_rest(b, infos.pop(i))
                hg, j, hh, kb = b
                if j == t and kb == 3:
                    pend_fin.append((fin_a(6 + hh, T), hg * 2 + hh, i + 6))
                if i + LOOK < len(blocks):
                    infos[i + LOOK] = emit_S(blocks[i + LOOK])
            for f2_, h_, _ in pend_fin:
                fin_b(f2_, T, h_, False, fbank=5)
        else:
            cst_ = {"cur": 0, "rec": 0}
            clist = [(s, g) for s in range(NSEQ_S) for g in range(PAST // T)]
            cs_b = [buf("cstage0"), buf("cstage1")]

            def c_acquire():
                idx = cst_["cur"]
                while cst_["rec"] < min(len(clist), idx + 2):
                    jj = cst_["rec"]
                    s_, g_ = clist[jj]
                    si = jj % 2
                    dma("sp", cstage[si][:], cckvT[l, s_, :, :, g_ * T:(g_ + 1) * T], "cstage%d" % si, [], [cs_b[si]])
                    dma("sp", krstage[si][64:96, :], ckrT[l, s_, :, g_ * T:(g_ + 1) * T], "cstage%d" % si, [], [cs_b[si]], batch=True)
                    cs_b[si].w = ("cstage%d" % si, P.dcount["cstage%d" % si])
                    cst_["rec"] += 1
                cst_["cur"] += 1
                return idx % 2

            stg = [(sq3, buf("sq3")), (cqn, buf("cqn"))]
            ngrp = PAST // T
            items = [(s_, g_) for s_ in range(NSEQ_S) for g_ in range(ngrp + 1)]
            gen = {}

            castinfo = {}

            def gen_cast(it):
                s_, g_ = it
                if g_ == ngrp:
                    return
                ci = c_acquire()
                n_ = s_ * ngrp + g_
                stt_, ccb_b = stg[n_ % 2]
                cbf = stt_[:, 0:2, :]
                act(cbf, cstage[ci][:], AF.Copy, [cs_b[ci]], [ccb_b])
                castinfo[it] = (ci, n_, cbf, ccb_b)

            def gen_rest(it):
                s_, g_ = it
                if g_ == ngrp:
                    gen[it] = (Kown, Vown, rk_own, kown_b, 1, TSAMP, True)
                    return
                ci, n_, cbf, ccb_b = castinfo.pop(it)
                si = n_ % NKV
                kv_generate(l, cbf, ccb_b, krstage[ci], cs_b[ci], T, kslot(si), vslot(si), rk_slot[si], kv_b[si],
                            set_ones=True, bcast_rope=False)
                gen[it] = (kslot(si), vslot(si), rk_slot[si], kv_b[si], 4, 128, False)

            def do_S(it):
                s_, g_ = it
                Ks, Vs, rks, src_b, nkb, kw, own = gen[it]
                bi = alloc_bank()
                for kb in range(nkb):
                    mm(banks[bi][0:kw, kb * 128:(kb + 1) * 128].rearrange("p (h q) -> p h q", h=8),
                       Ks[64:96, 0, kb * 128:kb * 128 + kw], Qv[64:96, :, s_ * 16:(s_ + 1) * 16], True, False,
                       [src_b, q_b], [bank_b[bi]])
                    for h in range(NH):
                        c0 = kb * 128 + h * 16
                        mm(banks[bi][0:kw, c0:c0 + 16], Ks[0:64, h, kb * 128:kb * 128 + kw], Qv[0:64, h, s_ * 16:(s_ + 1) * 16],
                           False, h == NH - 1, [src_b, q_b], [bank_b[bi]])
                f1 = alloc_ft()
                nc_ = nkb * 128
                tt("dve", ftmp[f1][0:kw, 0:nc_].rearrange("p (b h q) -> p b h q", b=nkb, h=8),
                   banks[bi][0:kw, 0:nc_].rearrange("p (b h q) -> p b h q", b=nkb, h=8),
                   rks[0:kw, 0:nkb, :].unsqueeze(3).broadcast_to([kw, nkb, 8, 16]), ALU.mult,
                   [bank_b[bi], src_b], [ftmp_b[f1]])
                pi = alloc_pt()
                if own:
                    act(PT[pi][0:kw, 0:nc_], ftmp[f1][0:kw, 0:nc_], AF.Exp, [ftmp_b[f1], cb], [pt_b[pi]],
                        bias=cst_sb[0:kw, 1 + s_:2 + s_])
                else:
                    act(PT[pi][0:kw, 0:nc_], ftmp[f1][0:kw, 0:nc_], AF.Exp, [ftmp_b[f1]], [pt_b[pi]])
                return pi

            def do_PV(it, pi):
                s_, g_ = it
                Ks, Vs, rks, src_b, nkb, kw, own = gen[it]
                ai = 6 + (s_ % 2)
                if g_ == 0:
                    memset("dve", banks[ai][0:65, 0:128], 0.0, [bank_b[ai]])
                for kb in range(nkb):
                    for h in range(NH):
                        c0 = kb * 128 + h * 16
                        mm(banks[ai][0:65, h * 16:(h + 1) * 16], Vs[0:kw, kb, h, :], PT[pi][0:kw, c0:c0 + 16], False, False,
                           [pt_b[pi], src_b] + ([vown_b] if own else []), [bank_b[ai]])
                if g_ == ngrp:
                    finalize(ai, 128, slice(s_ * 16, (s_ + 1) * 16), True)

            gen_cast(items[0])
            gen_rest(items[0])
            if len(items) > 1:
                gen_cast(items[1])
            for i, it in enumerate(items):
                pi = do_S(it)
                if i + 1 < len(items):
                    gen_rest(items[i + 1])
                if i + 2 < len(items):
                    gen_cast(items[i + 2])
                do_PV(it, pi)
        state["attn_mode"] = False
        state["nbank"] = 8

        _stage("attn")
        u_b = buf("u")
        for g in range(4):
            bi = alloc_bank()
            mm(banks[bi][:, :Tn], wpool(l, g), pooled[:, g, :Tn], True, True, [pooled_b, wsm_b], [bank_b[bi]])
            act(u_sb[:, g, :Tn], banks[bi][:, :Tn], AF.Identity, [bank_b[bi], cb], [u_b], scale=vcol(l, V_PS, g))

        for c in range(8):
            ws, wb = next_piece("mg%d" % c)
            wao = ws[:, 0:512].rearrange("p (h c) -> p h c", h=4)
            wpo = ws[:, 512:1024].rearrange("p (k c) -> p k c", k=4)
            wga = ws[:, 1024:2048].rearrange("p (k c) -> p k c", k=8)
            wgb = ws[:, 2048:3072].rearrange("p (k c) -> p k c", k=8)
            ba = alloc_bank()
            for hp in range(4):
                mm(banks[ba][:, :Tn], wao[:, hp, :], attn[:, hp, :Tn], hp == 0, hp == 3, [wb, attn_b], [bank_b[ba]])
            bb = alloc_bank()
            for kc in range(4):
                mm(banks[bb][:, :Tn], wpo[:, kc, :], u_sb[:, kc, :Tn], kc == 0, kc == 3, [wb, u_b], [bank_b[bb]])
            bga = alloc_bank()
            for kc in range(8):
                mm(banks[bga][:, :Tn], wga[:, kc, :], actbf[:, kc, :Tn], kc == 0, kc == 7, [wb, act_b[kc]], [bank_b[bga]])
            bgb = alloc_bank()
            for kc in range(8):
                mm(banks[bgb][:, :Tn], wgb[:, kc, :], actbf[:, kc, :Tn], kc == 0, kc == 7, [wb, act_b[kc]], [bank_b[bgb]])
            fa, fb = alloc_ft(), alloc_ft()
            act(ftmp[fa][:, :Tn], banks[bga][:, :Tn], AF.Tanh, [bank_b[bga]], [ftmp_b[fa]], scale=0.5)
            act(ftmp[fb][:, :Tn], banks[bgb][:, :Tn], AF.Tanh, [bank_b[bgb]], [ftmp_b[fb]], scale=0.5)
            stt("dve", ftmp[fa][:, :Tn], ftmp[fa][:, :Tn], 1.0, banks[ba][:, :Tn], ALU.add, ALU.mult,
                [ftmp_b[fa], bank_b[ba]], [ftmp_b[fa]])
            stt("dve", ftmp[fb][:, :Tn], ftmp[fb][:, :Tn], 1.0, banks[bb][:, :Tn], ALU.add, ALU.mult,
                [ftmp_b[fb], bank_b[bb]], [ftmp_b[fb]])
            tt("pool", m_sb[:, c, :Tn], ftmp[fa][:, :Tn], ftmp[fb][:, :Tn], ALU.add, [ftmp_b[fa], ftmp_b[fb]], [m_b[c]])

        _stage("merge")
        for j in range(2):
            ws, wb = next_piece("o%d" % j)
            w3 = ws[:, 0:4096].rearrange("p (k c) -> p k c", k=8)
            for cc in range(4):
                oc = j * 4 + cc
                bi = alloc_bank()
                for kc in range(8):
                    mm(banks[bi][:, :Tn], w3[:, kc, cc * 128:(cc + 1) * 128], m_sb[:, kc, :Tn], kc == 0, kc == 7,
                       [wb, m_b[kc]], [bank_b[bi]])
                stt("dve", xT[:, oc, :Tn], banks[bi][:, :Tn], 0.5, xT[:, oc, :Tn], ALU.mult, ALU.add,
                    [bank_b[bi], x_b[oc]], [x_b[oc]])

        _stage("wo")
        conv_step()
        rmsnorm_x(l, V_GMLP, Tn)
        conv_step()
        Prog.handoff(kv_b + kvs_b + [buf("Q")], a_b)
        for j in range(8):
            ws, wb = next_piece("up%d" % j)
            w3 = ws[:, 0:4096].rearrange("p (k c) -> p k c", k=8)
            for cc in range(4):
                oc = j * 4 + cc
                bi = alloc_bank()
                for kc in range(8):
                    mm(banks[bi][:, :Tn], w3[:, kc, cc * 128:(cc + 1) * 128], actbf[:, kc, :Tn], kc == 0, kc == 7,
                       [wb, act_b[kc]], [bank_b[bi]])
                f1 = alloc_ft()
                act(ftmp[f1][:, :Tn], banks[bi][:, :Tn], AF.Relu, [bank_b[bi]], [ftmp_b[f1]])
                tt("pool" if oc % 2 else "dve", aT[:, oc, :Tn], ftmp[f1][:, :Tn], ftmp[f1][:, :Tn], ALU.mult,
                   [ftmp_b[f1]], [a_b[oc]])
        for oc in range(8):
            ws, wb = next_piece("dn%d" % oc)
            w3 = ws[:, 0:4096].rearrange("p (k c) -> p k c", k=32)
            bi = alloc_bank()
            for kc in range(32):
                mm(banks[bi][:, :Tn], w3[:, kc, :], aT[:, kc, :Tn], kc == 0, kc == 31, [wb, a_b[kc]], [bank_b[bi]])
            tt("dve", xT[:, oc, :Tn], banks[bi][:, :Tn], xT[:, oc, :Tn], ALU.add, [bank_b[bi], x_b[oc]], [x_b[oc]])
            if l == 1:
                dst = ysT if samp else ypT[t]
                dma("pool", dst[:, oc, :Tn], xT[:, oc, :Tn], "st_y", [x_b[oc]], [])

    first = True
    try:
        for kind, t in tiles:
            for l in range(2):
                tile_layer(kind, t, l, not first)
                first = False
    except _Stop:
        pass

    store_keys = [k for k in P.dcount if k.startswith("st_")]
    fin = [(k, P.dcount[k]) for k in store_keys]
    P.lists["sp"].append((fin, None, None))

    semkeys = list(Prog.ENGS) + sorted(P.dcount.keys())
    import contextlib
    with contextlib.ExitStack() as es:
        sems = {k: es.enter_context(nc.semaphore("s_" + k)) for k in semkeys}
        block = es.enter_context(nc.Block())

        def mk(eng):
            def body(e):
                lst = P.lists[eng]
                if eng == "sp" and lst and lst[-1][1] is None:
                    fin_ = lst.pop()
                    P.emit(eng, e, sems)
                    for k, v in fin_[0]:
                        e.wait_ge(sems[k], v)
                else:
                    P.emit(eng, e, sems)
            return body
        block.tensor(mk("pe"))
        block.scalar(mk("act"))
        block.vector(mk("dve"))
        block.gpsimd(mk("pool"))
        block.sync(mk("sp"))
    return nc, P


def _lay_kc(W):
    K, C = W.shape
    return np.ascontiguousarray(W.reshape(K // 128, 128, C).transpose(1, 0, 2))


def _host_weights(inp):
    wfl = np.zeros((2, 128, TOT), np.float32)
    wsm = np.zeros((2, 128, NSM), np.float32)
    vecs = np.zeros((128, 2 * NV), np.float32)
    kr = OFF_KR + np.arange(32)
    kr_sw = OFF_KR + np.concatenate([np.arange(16, 32), np.arange(0, 16)])
    for l in range(2):
        W_in = np.asarray(inp["w_in"][l])

        def put(name, arr):
            o, s = POFF[name]
            a = arr.reshape(128, -1)
            assert a.shape[1] == s, (name, a.shape, s)
            wfl[l, :, o:o + s] = a
        idx0 = np.concatenate([np.arange(0, 384), kr_sw, kr, kr, kr_sw])
        put("in0", _lay_kc(W_in[:, idx0]))
        idx1 = np.concatenate([OFF_KV + np.arange(256), OFF_P + np.arange(256)])
        put("in1", _lay_kc(W_in[:, idx1]))
        put("in2", _lay_kc(W_in[:, OFF_P + 256:OFF_P + 512]))
        wq = np.asarray(inp["w_uq"][l])
        dsel = np.concatenate([np.arange(96), 64 + np.arange(16, 32), 64 + np.arange(0, 16)])
        wq2 = wq[:, :, dsel].reshape(384, 8 * 128)
        put("uq", _lay_kc(wq2))
        wao = np.asarray(inp["w_attn_out"][l]).reshape(4, 2, 64, 1024)
        wpo = np.asarray(inp["w_pool_out"][l])
        for c in range(8):
            cs = slice(c * 128, (c + 1) * 128)
            blk = np.zeros((128, 3072), np.float32)
            blk[:, 0:512] = wao[:, :, :, cs].transpose(1, 2, 0, 3).reshape(128, 512)
            blk[:, 512:1024] = _lay_kc(wpo[:, cs]).reshape(128, 512)
            blk[:, 1024:2048] = _lay_kc(W_in[:, OFF_GA + c * 128:OFF_GA + (c + 1) * 128]).reshape(128, 1024)
            blk[:, 2048:3072] = _lay_kc(W_in[:, OFF_GB + c * 128:OFF_GB + (c + 1) * 128]).reshape(128, 1024)
            put("mg%d" % c, blk)
        wo = np.asarray(inp["w_o"][l])
        for j in range(2):
            put("o%d" % j, _lay_kc(wo[:, j * 512:(j + 1) * 512]))
        wup = np.asarray(inp["w_up"][l])
        for j in range(8):
            put("up%d" % j, _lay_kc(wup[:, j * 512:(j + 1) * 512]))
        wdn = np.asarray(inp["w_down"][l])
        for j in range(8):
            put("dn%d" % j, _lay_kc(wdn[:, j * 128:(j + 1) * 128]))
        wkv = np.asarray(inp["w_ukv"][l])
        wsm[l, :, 0:1024] = _lay_kc(wkv[:, :, 0:64].reshape(256, 512)).reshape(128, 1024)
        wsm[l, :, 1024:2048] = _lay_kc(wkv[:, :, 64:128].reshape(256, 512)).reshape(128, 1024)
        wsm[l, :, 2048:2560] = np.asarray(inp["w_pool"][l]).transpose(1, 0, 2).reshape(128, 512)
        b = l * NV
        vecs[:, b + 0:b + 8] = np.asarray(inp["g_mix"][l]).reshape(8, 128).T
        vecs[:, b + 8:b + 16] = np.asarray(inp["g_mlp"][l]).reshape(8, 128).T
        vecs[:, b + 16:b + 19] = np.asarray(inp["g_qa"][l]).reshape(3, 128).T
        vecs[:, b + 19:b + 21] = np.asarray(inp["g_kva"][l]).reshape(2, 128).T
        vecs[:, b + 21:b + 25] = np.asarray(inp["pool_scale"][l]).reshape(4, 128).T
        vecs[0:96, b + 25] = np.asarray(inp["g_q"][l])
        vecs[0:96, b + 26] = np.asarray(inp["g_k"][l])
    return wfl, wsm, vecs


def _const_tables():
    cst = np.zeros((128, NCST), np.float32)
    cst[64:128, 0] = NEG
    for s in range(NSEQ_S):
        cst[:, 1 + s] = NEG
        cst[s * 16:(s + 1) * 16, 1 + s] = 0.0
    for g, w in enumerate(WINS):
        j = np.arange(16)
        cst[:, 5 + g * 16:5 + (g + 1) * 16] = (1.0 / np.minimum(w, j + 1).astype(np.float32))[None, :]
    half = RD // 2
    inv = np.power(np.float32(10000.0), -np.arange(half, dtype=np.float32) / np.float32(half)).astype(np.float32)
    rope = np.zeros((NT_FULL + 1, 128, 2, T), np.float32)
    for ti in range(NT_FULL + 1):
        if ti < NT_FULL:
            pos = (ti * T + np.arange(T)).astype(np.float32)
        else:
            pos = np.zeros(T, np.float32)
            pos[:TSAMP] = (PAST + (np.arange(TSAMP) % TS)).astype(np.float32)
        ang = (pos[None, :] * inv[:, None]).astype(np.float32)
        c = np.cos(ang).astype(np.float32)
        s = np.sin(ang).astype(np.float32)
        cos32 = np.concatenate([c, c], 0)
        sin32 = np.concatenate([-s, s], 0)
        for q in range(4):
            rope[ti, q * 32:(q + 1) * 32, 0] = cos32
            rope[ti, q * 32:(q + 1) * 32, 1] = sin32
    return cst, rope


_CACHE = {}


def _get_program(NT=NT_FULL, do_sample=True):
    key = (NT, do_sample)
    if key not in _CACHE:
        _CACHE[key] = build_program(NT, do_sample)
    return _CACHE[key]


def kernel(**inp):
    return _run(inp, NT_FULL, True)


def _run(inp, NT, do_sample, trace=False):
    inp = {k: np.asarray(v) for k, v in inp.items()}
    nc, _ = build_program(NT, do_sample)
    wfl, wsm, vecs = _host_weights(inp)
    cst, rope = _const_tables()
    xp = inp["x_prompt"]
    xs = inp["x_sample"]
    in_maps = []
    for c in range(8):
        xpT = np.ascontiguousarray(xp[c].reshape(NT_FULL, T, 8, 128).transpose(0, 3, 2, 1))
        xsT = np.ascontiguousarray(xs[4 * c:4 * c + 4].reshape(TSAMP, 8, 128).transpose(2, 1, 0))
        ck = inp["cache_ckv"][:, 4 * c:4 * c + 4]
        cckvT = np.ascontiguousarray(ck.reshape(2, 4, PAST, 2, 128).transpose(0, 1, 4, 3, 2))
        ckrT = np.ascontiguousarray(inp["cache_krope"][:, 4 * c:4 * c + 4].transpose(0, 1, 3, 2))
        sp = inp["state_pool"][:, 4 * c:4 * c + 4]
        spT = np.ascontiguousarray(sp.reshape(2, 4, 15, 4, 128).transpose(0, 4, 3, 1, 2))
        in_maps.append({"xpT": xpT, "xsT": xsT, "cckvT": cckvT, "ckrT": ckrT, "spT": spT, "wfl": wfl, "wsm": wsm,
                        "vecs": vecs, "cst": cst, "ropeT": rope})
    res = run_bass_kernel_spmd(nc, in_maps, core_ids=list(range(8)), **({"trace": True} if trace else {}))
    R = res.results
    y_p = np.stack([R[c]["ypT"].transpose(0, 3, 2, 1).reshape(SEQ, D) for c in range(8)])
    y_s = np.concatenate([R[c]["ysT"].transpose(2, 1, 0).reshape(4, TS, D) for c in range(8)])
    ckv_p = np.stack([R[c]["ckvpT"].transpose(0, 1, 4, 3, 2).reshape(2, SEQ, KVL) for c in range(8)], 1)
    kr_p = np.stack([R[c]["krpT"].transpose(0, 1, 3, 2).reshape(2, SEQ, RD) for c in range(8)], 1)
    pool_p = np.stack([R[c]["poolpT"].transpose(0, 3, 2, 1).reshape(2, 15, PW) for c in range(8)], 1)
    ckv_s = np.concatenate([R[c]["ckvsT"].transpose(0, 3, 2, 1).reshape(2, 4, TS, KVL) for c in range(8)], 1)
    kr_s = np.concatenate([R[c]["krsT"].transpose(0, 2, 1).reshape(2, 4, TS, RD) for c in range(8)], 1)
    pool_s = np.concatenate([R[c]["poolsT"].transpose(0, 3, 4, 2, 1).reshape(2, 4, 15, PW) for c in range(8)], 1)
    outs = (y_p, y_s, ckv_p, kr_p, pool_p, ckv_s, kr_s, pool_s)
    outs = tuple(np.ascontiguousarray(o, dtype=np.float32) for o in outs)
    if trace:
        return outs, res
    return outs
```
